# Optimizing a Trainium2 kernel written in Bass

```python
import math
import jax, jax.numpy as jnp
from jax import lax
import numpy as np

D_MODEL = 1024
BATCH = 8
SEQ = 2048
DEPTH = 2

N_MIXERS = 2
N_A_LAYERS = (DEPTH + 1) // 2
N_B_LAYERS = DEPTH // 2
RMS_EPS = 1e-6

D_FF = 2816
FFN_RES = 0.5

D_RNN = 1280
N_RNN_BLOCKS = 10
RNN_BLOCK = D_RNN // N_RNN_BLOCKS
CONV_WIDTH = 4
LRU_C = 8.0

HEAD_DIM = 64
HEADS_PER_GROUP = D_MODEL // HEAD_DIM
DILATION_GROUPS = ((128, 1), (512, 4), (2048, 16))
N_GROUPS = len(DILATION_GROUPS)
N_ATT_HEADS = N_GROUPS * HEADS_PER_GROUP
QKV_WIDTH = 3 * N_ATT_HEADS * HEAD_DIM
ATT_OUT_WIDTH = HEADS_PER_GROUP * HEAD_DIM
N_BUCKETS = 32
MAX_DISTANCE = 2048

kernel_name = 'hybrid_rglru_dilated_attn_macaron'


def rms_norm(x, g):
    xf = x.astype(jnp.float32)
    y = xf * lax.rsqrt(jnp.mean(xf * xf, axis=-1, keepdims=True) + RMS_EPS)
    return (y * g.astype(jnp.float32)).astype(x.dtype)


def swiglu(x, w_in, w_out):
    gate, up = jnp.split(x @ w_in, 2, axis=-1)
    return (jax.nn.silu(gate) * up) @ w_out


def rglru_mixer(x, w_in, conv_w, conv_b, w_a, b_a, w_x, b_x, lam, w_out):
    B, S, _ = x.shape
    gate, u = jnp.split(x @ w_in, 2, axis=-1)
    up = jnp.pad(u, ((0, 0), (CONV_WIDTH - 1, 0), (0, 0)))
    conv = conv_b
    for k in range(CONV_WIDTH):
        conv = conv + up[:, k:k + S] * conv_w[k]
    uf = conv.astype(jnp.float32)
    ub = uf.reshape(B, S, N_RNN_BLOCKS, RNN_BLOCK)
    r = jax.nn.sigmoid(jnp.einsum('bsnc,ncd->bsnd', ub, w_a.astype(jnp.float32)).reshape(B, S, D_RNN) + b_a.astype(jnp.float32))
    i = jax.nn.sigmoid(jnp.einsum('bsnc,ncd->bsnd', ub, w_x.astype(jnp.float32)).reshape(B, S, D_RNN) + b_x.astype(jnp.float32))
    log_a = -LRU_C * r * jax.nn.softplus(-lam.astype(jnp.float32))
    a = jnp.exp(log_a)
    b = jnp.sqrt(-jnp.expm1(2.0 * log_a)) * (i * uf)

    def combine(left, right):
        a1, b1 = left
        a2, b2 = right
        return a1 * a2, a2 * b1 + b2

    _, h = lax.associative_scan(combine, (a, b), axis=1)
    y = h.astype(x.dtype) * jax.nn.gelu(gate)
    return y @ w_out


def t5_causal_bucket(dist):
    max_exact = N_BUCKETS // 2
    n = jnp.maximum(dist, 0)
    nf = jnp.maximum(n, 1).astype(jnp.float32)
    large = max_exact + (jnp.log(nf / max_exact) / math.log(MAX_DISTANCE / max_exact) * (N_BUCKETS - max_exact)).astype(jnp.int32)
    large = jnp.minimum(large, N_BUCKETS - 1)
    return jnp.where(n < max_exact, n, large)


def dilated_group(q, k, v, bias_tbl, window, dilation):
    B, S, H, Dh = q.shape
    d = dilation
    blk = window // dilation
    span = d * blk
    Sp = ((S + span - 1) // span) * span
    pad = ((0, 0), (0, Sp - S), (0, 0), (0, 0))
    L = Sp // d
    nb = L // blk

    def to_sub(t):
        t = jnp.pad(t, pad).reshape(B, L, d, H, Dh).transpose(0, 2, 1, 3, 4)
        return t.reshape(B, d, nb, blk, H, Dh)

    def with_prev(t):
        prev = jnp.pad(t, ((0, 0), (0, 0), (1, 0), (0, 0), (0, 0), (0, 0)))[:, :, :-1]
        return jnp.concatenate([prev, t], axis=3)

    qs = to_sub(q).astype(jnp.float32)
    ks = with_prev(to_sub(k)).astype(jnp.float32)
    vs = with_prev(to_sub(v))
    scores = jnp.einsum('brnqhc,brnkhc->brnhqk', qs, ks) * (HEAD_DIM ** -0.5)

    qi = jnp.arange(blk)[:, None]
    kj = jnp.arange(2 * blk)[None, :]
    dist = qi - kj + blk
    band = (dist >= 0) & (dist <= blk)
    key_exists = (jnp.arange(nb)[:, None, None] > 0) | (kj[None] >= blk)
    mask = (band[None] & key_exists)[:, None]
    bias = bias_tbl.astype(jnp.float32)[t5_causal_bucket(dist * d)].transpose(2, 0, 1)
    scores = jnp.where(mask, scores + bias, -jnp.inf)
    lse = jax.nn.logsumexp(scores, axis=-1)
    p = jnp.exp(scores - lse[..., None])
    o = jnp.einsum('brnhqk,brnkhc->brnqhc', p, vs.astype(jnp.float32))
    o = o.reshape(B, d, L, H, Dh).transpose(0, 2, 1, 3, 4).reshape(B, Sp, H, Dh)[:, :S]
    lse = lse.transpose(0, 1, 2, 4, 3).reshape(B, d, L, H).transpose(0, 2, 1, 3).reshape(B, Sp, H)[:, :S]
    return o, lse


def dilated_attention(x, w_qkv, q_gain, k_gain, rel_bias, w_o):
    B, S, _ = x.shape
    qkv = (x @ w_qkv).reshape(B, S, 3, N_GROUPS, HEADS_PER_GROUP, HEAD_DIM)
    q = rms_norm(qkv[:, :, 0], q_gain)
    k = rms_norm(qkv[:, :, 1], k_gain)
    v = qkv[:, :, 2]
    outs, lses = [], []
    for g, (window, dilation) in enumerate(DILATION_GROUPS):
        tbl = rel_bias[:, g * HEADS_PER_GROUP:(g + 1) * HEADS_PER_GROUP]
        o, lse = dilated_group(q[:, :, g], k[:, :, g], v[:, :, g], tbl, window, dilation)
        outs.append(o)
        lses.append(lse)
    wts = jax.nn.softmax(jnp.stack(lses, axis=0), axis=0)
    o = jnp.einsum('gbsh,gbshc->bshc', wts, jnp.stack(outs, axis=0))
    return o.reshape(B, S, ATT_OUT_WIDTH).astype(x.dtype) @ w_o


def setup_inputs(seed: int = 0) -> dict:
    key = jax.random.key(seed)
    ks = jax.random.split(key, 24)
    nrm = jax.random.normal
    f32 = jnp.float32
    x = nrm(ks[0], (BATCH, SEQ, D_MODEL), f32)
    norm_g = 1.0 + 0.05 * nrm(ks[1], (DEPTH, 3, D_MODEL), f32)
    ffn_w_in = nrm(ks[2], (DEPTH, 2, D_MODEL, 2 * D_FF), f32) * D_MODEL ** -0.5
    ffn_w_out = nrm(ks[3], (DEPTH, 2, D_FF, D_MODEL), f32) * D_FF ** -0.5
    rnn_w_in = nrm(ks[4], (N_A_LAYERS, D_MODEL, 2 * D_RNN), f32) * D_MODEL ** -0.5
    rnn_conv_w = nrm(ks[5], (N_A_LAYERS, CONV_WIDTH, D_RNN), f32) * CONV_WIDTH ** -0.5
    rnn_conv_b = 0.02 * nrm(ks[6], (N_A_LAYERS, D_RNN), f32)
    rnn_w_a = nrm(ks[7], (N_A_LAYERS, N_RNN_BLOCKS, RNN_BLOCK, RNN_BLOCK), f32) * RNN_BLOCK ** -0.5
    rnn_b_a = 0.02 * nrm(ks[8], (N_A_LAYERS, D_RNN), f32)
    rnn_w_x = nrm(ks[9], (N_A_LAYERS, N_RNN_BLOCKS, RNN_BLOCK, RNN_BLOCK), f32) * RNN_BLOCK ** -0.5
    rnn_b_x = 0.02 * nrm(ks[10], (N_A_LAYERS, D_RNN), f32)
    a_c = jax.random.uniform(ks[11], (N_A_LAYERS, D_RNN), f32, 0.9, 0.999)
    a_base = a_c ** (1.0 / LRU_C)
    rnn_lambda = jnp.log(a_base) - jnp.log1p(-a_base)
    rnn_w_out = nrm(ks[12], (N_A_LAYERS, D_RNN, D_MODEL), f32) * D_RNN ** -0.5
    att_w_qkv = nrm(ks[13], (N_B_LAYERS, D_MODEL, QKV_WIDTH), f32) * D_MODEL ** -0.5
    att_q_gain = 1.0 + 0.05 * nrm(ks[14], (N_B_LAYERS, HEAD_DIM), f32)
    att_k_gain = 1.0 + 0.05 * nrm(ks[15], (N_B_LAYERS, HEAD_DIM), f32)
    att_w_o = nrm(ks[16], (N_B_LAYERS, ATT_OUT_WIDTH, D_MODEL), f32) * ATT_OUT_WIDTH ** -0.5
    rel_bias = 0.5 * nrm(ks[17], (N_BUCKETS, N_ATT_HEADS), f32)
    return {'x': x, 'norm_g': norm_g, 'ffn_w_in': ffn_w_in, 'ffn_w_out': ffn_w_out,
            'rnn_w_in': rnn_w_in, 'rnn_conv_w': rnn_conv_w, 'rnn_conv_b': rnn_conv_b,
            'rnn_w_a': rnn_w_a, 'rnn_b_a': rnn_b_a, 'rnn_w_x': rnn_w_x, 'rnn_b_x': rnn_b_x,
            'rnn_lambda': rnn_lambda, 'rnn_w_out': rnn_w_out,
            'att_w_qkv': att_w_qkv, 'att_q_gain': att_q_gain, 'att_k_gain': att_k_gain,
            'att_w_o': att_w_o, 'rel_bias': rel_bias}


def reference(x, norm_g, ffn_w_in, ffn_w_out, rnn_w_in, rnn_conv_w, rnn_conv_b,
              rnn_w_a, rnn_b_a, rnn_w_x, rnn_b_x, rnn_lambda, rnn_w_out,
              att_w_qkv, att_q_gain, att_k_gain, att_w_o, rel_bias):
    for layer in range(DEPTH):
        g = norm_g[layer]
        x = x + FFN_RES * swiglu(rms_norm(x, g[0]), ffn_w_in[layer, 0], ffn_w_out[layer, 0])
        h = rms_norm(x, g[1])
        j = layer // N_MIXERS
        if layer % N_MIXERS == 0:
            x = x + rglru_mixer(h, rnn_w_in[j], rnn_conv_w[j], rnn_conv_b[j], rnn_w_a[j], rnn_b_a[j],
                                rnn_w_x[j], rnn_b_x[j], rnn_lambda[j], rnn_w_out[j])
        else:
            x = x + dilated_attention(h, att_w_qkv[j], att_q_gain[j], att_k_gain[j], rel_bias, att_w_o[j])
        x = x + FFN_RES * swiglu(rms_norm(x, g[2]), ffn_w_in[layer, 1], ffn_w_out[layer, 1])
    return x
```

```python
import math
from contextlib import ExitStack

import numpy as np
import concourse.bass as bass
import concourse.mybir as mybir
from concourse.bass_utils import run_bass_kernel_spmd

F32 = mybir.dt.float32
BF16 = mybir.dt.bfloat16
AF = mybir.ActivationFunctionType
ALU = mybir.AluOpType

D = 1024
S = 2048
KD = D // 128
NT = S // 512
DFF = 2816
NFC = DFF // 128
DRNN = 1280
NRC = DRNN // 128
EPS = 1e-6
NEG = -30000.0
DIL = (1, 4, 16)
N_CORES = 8

PV_G = 0
PV_CW = 48
PV_CB = 88
PV_BA = 98
PV_BX = 108
PV_LAM = 118
PV_QG = 128
PV_KG = 129
PV_N = 130


class Prog:
    ENG = ("pe", "act", "dve", "pool", "sp")
    CENG = ("pe", "act", "dve", "pool")

    def __init__(self, nc, es):
        self.nc = nc
        self.es = es
        self.q = {e: [] for e in self.ENG}
        self.seq = {e: 0 for e in self.ENG}
        self.waited = {e: {} for e in self.ENG}
        self.lastw = {}
        self.readers = {}
        self.sems = {}
        self.dcount = {}
        self.needed = {e: set() for e in self.CENG}
        for e in self.CENG:
            self.sems[e] = es.enter_context(nc.semaphore("s_" + e))

    def _sem(self, sk):
        if sk not in self.sems:
            self.sems[sk] = self.es.enter_context(self.nc.semaphore("d_" + sk))
        return self.sems[sk]

    def _wait(self, eng, sk, v):
        if self.waited[eng].get(sk, 0) >= v:
            return
        self.waited[eng][sk] = v
        self.q[eng].append(("wait", sk, v))
        if sk in self.needed:
            self.needed[sk].add(v)

    def _deps(self, eng, reads, writes):
        deps = {}

        def add(d):
            if d is None:
                return
            sk, v = d
            if deps.get(sk, 0) < v:
                deps[sk] = v

        for k in reads:
            add(self.lastw.get(k))
        for k in writes:
            add(self.lastw.get(k))
            for sk, v in self.readers.get(k, {}).items():
                add((sk, v))
        for sk, v in deps.items():
            if sk == eng and eng == "pe":
                continue
            self._wait(eng, sk, v)

    def _mark(self, sk, val, reads, writes):
        for k in writes:
            self.lastw[k] = (sk, val)
            self.readers[k] = {}
        for k in reads:
            r = self.readers.setdefault(k, {})
            if r.get(sk, 0) < val:
                r[sk] = val

    def op(self, eng, name, kw, reads=(), writes=(), sig=True):
        self._deps(eng, reads, writes)
        if sig:
            self.seq[eng] += 1
            oid = self.seq[eng]
        else:
            oid = None
        self.q[eng].append(("op", (name, kw), oid, sig))
        self._mark(eng, self.seq[eng] if sig else self.seq[eng] + 1, reads, writes)

    def barrier(self):
        for eng in self.ENG:
            for f in self.CENG:
                if f == eng or self.seq[f] == 0:
                    continue
                self._wait(eng, f, self.seq[f])

    def dma(self, eng, stream, out, in_, reads=(), writes=()):
        self._sem(stream)
        self._deps(eng, reads, writes)
        self.dcount[stream] = self.dcount.get(stream, 0) + 16
        self.q[eng].append(("dma", out, in_, stream))
        self._mark(stream, self.dcount[stream], reads, writes)

    def final_wait(self, eng, stream):
        self.q[eng].append(("wait", stream, self.dcount[stream]))

    def _rank(self, eng):
        ids = sorted(self.needed[eng])
        return {v: i + 1 for i, v in enumerate(ids)}

    def emit(self, eng, e):
        my = self.sems.get(eng)
        ranks = {f: self._rank(f) for f in self.CENG}
        myneed = self.needed.get(eng, set())
        for item in self.q[eng]:
            if item[0] == "wait":
                sk, v = item[1], item[2]
                if sk in ranks:
                    v = ranks[sk][v]
                e.wait_ge(self._sem(sk), v)
            elif item[0] == "op":
                name, kw = item[1]
                ins = getattr(e, name)(**kw)
                if item[2] is not None and item[2] in myneed:
                    ins.then_inc(my, 1)
            else:
                e.dma_start(out=item[1], in_=item[2]).then_inc(self._sem(item[3]), 16)


class Ring:
    def __init__(self, n):
        self.n = n
        self.i = -1

    def next(self):
        self.i = (self.i + 1) % self.n
        return self.i


def build_program():
    nc = bass.Bass("TRN2", target_bir_lowering=False)

    def din(name, shape):
        return nc.dram_tensor(name, list(shape), F32, kind="ExternalInput").ap()

    xT_d = din("xT", [D, S])
    pv_d = din("pv", [128, PV_N])
    bt_d = din("bt", [128, 48, 256])
    fwi_d = din("ffn_w_in", [4 * D, 2 * DFF])
    fwo_d = din("ffn_w_out", [4 * DFF, D])
    rwi_d = din("rnn_w_in", [D, 2 * DRNN])
    rwa_d = din("rnn_w_a", [DRNN, 128])
    rwx_d = din("rnn_w_x", [DRNN, 128])
    rwo_d = din("rnn_w_out", [DRNN, D])
    aqkv_d = din("att_w_qkv", [D, 9216])
    awo_d = din("att_w_o", [D, D])
    out_d = nc.dram_tensor("outT", [D, S], F32, kind="ExternalOutput").ap()

    with ExitStack() as es:
        P = Prog(nc, es)

        def sb(name, shape, dt):
            return es.enter_context(nc.sbuf_tensor(name, list(shape), dt))

        xT = sb("xT_sb", [128, KD, S], F32)
        xn = sb("xn_sb", [128, KD, S], BF16)
        pv = sb("pv_sb", [128, PV_N], F32)
        dv = sb("dv_sb", [128, 64], F32)
        ones_bf = sb("ones_bf", [128, 128], BF16)
        bd_bf = sb("bd_bf", [128, 128], BF16)
        banks = [es.enter_context(nc.psum_tensor("bank%d" % i, [128, 512], F32)) for i in range(8)]
        DV_EPS, DV_ONE, DV_NLN8, DV_NEG1, DV_HBA, DV_HBX, DV_C4, DV_TMP = 0, 1, 2, 3, 4, 14, 24, 34

        bank_rr = Ring(8)

        def BK(i):
            return ("bank", i)

        def MM(out, lhsT, rhs, start, stop, reads, writes, sig=True):
            P.op("pe", "matmul", dict(out=out, lhsT=lhsT, rhs=rhs, start=start, stop=stop), reads, writes, sig)

        def ACT(out, in_, func, reads, writes, bias=None, scale=None):
            kw = dict(out=out, in_=in_, func=func)
            if bias is not None:
                kw["bias"] = bias
            if scale is not None:
                kw["scale"] = scale
            P.op("act", "activation", kw, reads, writes)

        def TS(out, in0, s1, s2, op0, op1, reads, writes):
            kw = dict(out=out, in0=in0, scalar1=s1, scalar2=s2, op0=op0)
            if op1 is not None:
                kw["op1"] = op1
            P.op("dve", "tensor_scalar", kw, reads, writes)

        def STT(out, in0, scalar, in1, op0, op1, reads, writes):
            P.op("dve", "scalar_tensor_tensor", dict(out=out, in0=in0, scalar=scalar, in1=in1, op0=op0, op1=op1),
                 reads, writes)

        def TT(out, in0, in1, op, reads, writes):
            P.op("dve", "tensor_tensor", dict(out=out, in0=in0, in1=in1, op=op), reads, writes)

        def RECIP(out, in_, reads, writes):
            P.op("dve", "reciprocal", dict(out=out, in_=in_), reads, writes)

        def RECIP_FAST(out, in_, reads, writes):
            P.op("dve", "reciprocal_approx_fast", dict(out=out, in_=in_), reads, writes)

        def MEMSET(ap, val, writes):
            P.op("dve", "memset", dict(ap=ap, constant=val), (), writes)

        def col(t, c, n=1):
            return t[:, c:c + n]

        xT_dv = xT_d.rearrange("(k p) t -> p k t", p=128)
        for k in range(KD):
            P.dma("sp", "xin%d" % k, xT[:, k, :], xT_dv[:, k, :],
                  writes=[("x", k, t) for t in range(NT)])
        P.dma("sp", "pvin", pv[:], pv_d, writes=["pv"])

        MEMSET(ones_bf[:], 1.0, ["ones"])
        MEMSET(bd_bf[:], 0.0, ["bd"])
        MEMSET(bd_bf[0:64, 0:64], 1.0, ["bd"])
        MEMSET(bd_bf[64:128, 64:128], 1.0, ["bd"])
        MEMSET(col(dv, DV_EPS), EPS, ["dvc"])
        MEMSET(col(dv, DV_ONE), 1.0, ["dvc"])
        MEMSET(col(dv, DV_NLN8), -math.log(8.0), ["dvc"])
        MEMSET(col(dv, DV_NEG1), -1.0, ["dvc"])
        TS(col(dv, DV_HBA, 20), col(pv, PV_BA, 20), 0.5, None, ALU.mult, None, ["pv"], ["dvh"])
        ACT(col(dv, DV_TMP, 10), col(pv, PV_LAM, 10), AF.Exp, ["pv"], ["dvt"], scale=-1.0)
        ACT(col(dv, DV_TMP + 10, 10), col(dv, DV_TMP, 10), AF.Ln, ["dvt", "dvc"], ["dvt2"], bias=col(dv, DV_ONE), scale=1.0)
        TS(col(dv, DV_C4, 10), col(dv, DV_TMP + 10, 10), 2.0, None, ALU.mult, None, ["dvt2"], ["dvc4"])

        TSL = [slice(t * 512, (t + 1) * 512) for t in range(NT)]

        def rmsnorm(gcol, tmp):
            sq, lnt, rs = tmp
            sq_rr, ln_rr = Ring(4), Ring(2)
            for t in range(NT):
                ts = TSL[t]
                b = bank_rr.next()
                for k in range(KD):
                    s = sq_rr.next()
                    if k % 2 == 0:
                        ACT(sq[:, s, :], xT[:, k, ts], AF.Square, [("x", k, t)], [("sq", s)])
                    else:
                        TT(sq[:, s, :], xT[:, k, ts], xT[:, k, ts], ALU.mult, [("x", k, t)], [("sq", s)])
                    MM(banks[b][:, :], ones_bf[:, :], sq[:, s, :], k == 0, k == KD - 1,
                       [("sq", s), "ones"], [BK(b)])
                l = ln_rr.next()
                ACT(lnt[:, l, :], banks[b][:, :], AF.Ln, ["dvc"], [BK(b), ("lnt", l)], bias=col(dv, DV_EPS), scale=1.0 / D)
                ACT(rs[:, l, :], lnt[:, l, :], AF.Exp, [("lnt", l)], [("rs", l)], scale=-0.5)
                for k in range(KD):
                    STT(xn[:, k, ts], xT[:, k, ts], col(pv, gcol + k), rs[:, l, :], ALU.mult, ALU.mult,
                        [("x", k, t), ("rs", l), "pv"], [("xn", k, t)])

        def ffn(idx, gcol):
            with ExitStack() as fs:
                def fsb(name, shape, dt):
                    return fs.enter_context(nc.sbuf_tensor("%s_%d" % (name, idx), list(shape), dt))
                NSL = 2
                CPS = NFC // NSL
                gT = fsb("gT", [128, CPS, S], BF16)
                win = fsb("win", [128, 4, 2, KD, 128], BF16)
                wout = fsb("wout", [128, 4, CPS, 128], BF16)
                sq = fsb("sq", [128, 4, 512], BF16)
                lnt = fsb("lnt", [128, 2, 512], F32)
                rs = fsb("rs", [128, 2, 512], F32)
                sg = fsb("sg", [128, 3, 512], F32)
                rmsnorm(gcol, (sq, lnt, rs))
                if DBG.get("ffn_stop") == "norm":
                    P.barrier()
                    return
                wi_v = fwi_d[idx * D:(idx + 1) * D, :].rearrange("(k p) f -> p k f", p=128)
                wo_v = fwo_d[idx * DFF:(idx + 1) * DFF, :].rearrange("(c p) d -> p c d", p=128)
                win_rr, wout_rr, sg_rr = Ring(4), Ring(4), Ring(3)
                for sl in range(NSL):
                    for ci in range(CPS):
                        c = sl * CPS + ci
                        ws = win_rr.next()
                        for gu in range(2):
                            c0 = gu * DFF + c * 128
                            P.dma("pool", "win%d_%d" % (ws, gu), win[:, ws, gu, :, :], wi_v[:, :, c0:c0 + 128],
                                  writes=[("win", ws, gu)])
                        for t in range(NT):
                            ts = TSL[t]
                            bg = bank_rr.next()
                            bu = bank_rr.next()
                            for gu, b in ((0, bg), (1, bu)):
                                for k in range(KD):
                                    MM(banks[b][:, :], win[:, ws, gu, k, :], xn[:, k, ts], k == 0, k == KD - 1,
                                       [("win", ws, gu), ("xn", k, t)], [BK(b)], sig=(k == KD - 1))
                            s = sg_rr.next()
                            ACT(sg[:, s, :], banks[bg][:, :], AF.Silu, [], [BK(bg), ("sg", s)])
                            TT(gT[:, ci, ts], sg[:, s, :], banks[bu][:, :], ALU.mult, [("sg", s)], [BK(bu), ("gT", ci, t)])
                    if DBG.get("ffn_stop") == "up":
                        continue
                    for dc in range(KD):
                        ws = wout_rr.next()
                        for hf, (c_lo, c_hi) in enumerate(((0, 6), (6, CPS))):
                            P.dma("pool", "wout%d_%d" % (ws, hf), wout[:, ws, c_lo:c_hi, :],
                                  wo_v[:, sl * CPS + c_lo:sl * CPS + c_hi, dc * 128:(dc + 1) * 128],
                                  writes=[("wout", ws, hf)])
                        for t in range(NT):
                            ts = TSL[t]
                            b = bank_rr.next()
                            for ci in range(CPS):
                                MM(banks[b][:, :], wout[:, ws, ci, :], gT[:, ci, ts], ci == 0, ci == CPS - 1,
                                   [("wout", ws, 0 if ci < 6 else 1), ("gT", ci, t)], [BK(b)], sig=(ci == CPS - 1))
                            STT(xT[:, dc, ts], banks[b][:, :], 0.5, xT[:, dc, ts], ALU.mult, ALU.add,
                                [], [BK(b), ("x", dc, t)])
                P.barrier()

        def rnn(gcol):
            with ExitStack() as fs:
                def fsb(name, shape, dt):
                    return fs.enter_context(nc.sbuf_tensor("r_" + name, list(shape), dt))
                sq = fsb("sq", [128, 4, 512], BF16)
                lnt = fsb("lnt", [128, 2, 512], F32)
                rs = fsb("rs", [128, 2, 512], F32)
                wgu = fsb("wgu", [128, 3, 2, KD, 128], BF16)
                wax = fsb("wax", [128, NRC, 2, 128], BF16)
                wo = fsb("wo", [128, NRC, D], BF16)
                gg = fsb("gg", [128, 3, 512], F32)
                ub = fsb("ub", [128, 2, 516], F32)
                cv = fsb("cv", [128, 2, 512], F32)
                cvb = fsb("cvb", [128, 2, 512], BF16)
                rp = fsb("rp", [128, 2, 512], F32)
                ip = fsb("ip", [128, 2, 512], F32)
                uu = fsb("uu", [128, 2, 512], F32)
                ww = fsb("ww", [128, 2, 512], F32)
                yb = fsb("yb", [128, 2, NRC, 512], BF16)
                ucar = fsb("ucar", [128, NRC, 4], F32)
                hcar = fsb("hcar", [128, NRC], F32)
                rmsnorm(gcol, (sq, lnt, rs))
                MEMSET(ucar[:, :, :], 0.0, [("ucar", n) for n in range(NRC)])
                wa_v = rwa_d.rearrange("(n p) d -> p n d", p=128)
                wx_v = rwx_d.rearrange("(n p) d -> p n d", p=128)
                wo_v = rwo_d.rearrange("(n p) d -> p n d", p=128)
                for hf in range(2):
                    ns = slice(hf * 5, hf * 5 + 5)
                    P.dma("pool", "rwa%d" % hf, wax[:, ns, 0, :], wa_v[:, ns, :], writes=[("wax", 0, hf)])
                    P.dma("pool", "rwx%d" % hf, wax[:, ns, 1, :], wx_v[:, ns, :], writes=[("wax", 1, hf)])
                    P.dma("pool", "rwo%d" % hf, wo[:, ns, :], wo_v[:, ns, :], writes=[("wo", hf)])
                wi_v = rwi_d.rearrange("(k p) f -> p k f", p=128)
                NU = NT * NRC
                units = [(t, n) for t in range(NT) for n in range(NRC)]

                def load_w(k):
                    t, n = units[k]
                    ws = k % 3
                    for gu in range(2):
                        c0 = gu * DRNN + n * 128
                        P.dma("pool", "rwgu%d_%d" % (ws, gu), wgu[:, ws, gu, :, :], wi_v[:, :, c0:c0 + 128],
                              writes=[("wgu", ws, gu)])

                def stage_a(k):
                    t, n = units[k]
                    pb, ws, ts = k % 2, k % 3, TSL[t]
                    b = bank_rr.next()
                    for kd in range(KD):
                        MM(banks[b][:, :], wgu[:, ws, 0, kd, :], xn[:, kd, ts], kd == 0, kd == KD - 1,
                           [("wgu", ws, 0), ("xn", kd, t)], [BK(b)], sig=(kd == KD - 1))
                    ACT(gg[:, k % 3, :], banks[b][:, :], AF.Gelu_apprx_tanh, [], [BK(b), ("gg", k % 3)])
                    b = bank_rr.next()
                    for kd in range(KD):
                        MM(banks[b][:, :], wgu[:, ws, 1, kd, :], xn[:, kd, ts], kd == 0, kd == KD - 1,
                           [("wgu", ws, 1), ("xn", kd, t)], [BK(b)], sig=(kd == KD - 1))
                    ACT(ub[:, pb, 3:515], banks[b][:, :], AF.Copy, [], [BK(b), ("ub", pb)])
                    ACT(ub[:, pb, 0:3], ucar[:, n, 0:3], AF.Copy, [("ucar", n)], [("ub", pb)])
                    ACT(ucar[:, n, 0:3], ub[:, pb, 512:515], AF.Copy, [("ub", pb)], [("ucar", n)])

                def stage_c1(k):
                    t, n = units[k]
                    pb = k % 2
                    cw = PV_CW + n * 4
                    TS(cv[:, pb, :], ub[:, pb, 0:512], col(pv, cw), col(pv, PV_CB + n), ALU.mult, ALU.add,
                       [("ub", pb), "pv"], [("cv", pb)])
                    for kk in range(1, 4):
                        STT(cv[:, pb, :], ub[:, pb, kk:kk + 512], col(pv, cw + kk), cv[:, pb, :], ALU.mult, ALU.add,
                            [("ub", pb), "pv"], [("cv", pb)])

                def stage_c2(k):
                    t, n = units[k]
                    pb = k % 2
                    ACT(cvb[:, pb, :], cv[:, pb, :], AF.Copy, [("cv", pb)], [("cvb", pb)])
                    br = bank_rr.next()
                    MM(banks[br][:, :], wax[:, n, 0, :], cvb[:, pb, :], True, True, [("wax", 0, n // 5), ("cvb", pb)], [BK(br)])
                    bi = bank_rr.next()
                    MM(banks[bi][:, :], wax[:, n, 1, :], cvb[:, pb, :], True, True, [("wax", 1, n // 5), ("cvb", pb)], [BK(bi)])
                    ACT(rp[:, pb, :], banks[br][:, :], AF.Tanh, ["dvh"], [BK(br), ("rp", pb)], bias=col(dv, DV_HBA + n), scale=0.5)
                    ACT(ip[:, pb, :], banks[bi][:, :], AF.Tanh, ["dvh"], [BK(bi), ("ip", pb)], bias=col(dv, DV_HBX + n), scale=0.5)
                    ACT(uu[:, pb, :], rp[:, pb, :], AF.Tanh, [("rp", pb), "dvc4"], [("uu", pb)],
                        bias=col(dv, DV_C4 + n), scale=col(dv, DV_C4 + n))
                    ACT(ww[:, pb, :], uu[:, pb, :], AF.Identity, [("uu", pb), "dvc"], [("ww", pb)], bias=col(dv, DV_ONE), scale=1.0)

                def stage_e1(k):
                    pb = k % 2
                    RECIP(ww[:, pb, :], ww[:, pb, :], [("ww", pb)], [("ww", pb)])

                def stage_e2(k):
                    t, n = units[k]
                    pb = k % 2
                    ACT(rp[:, pb, :], ww[:, pb, :], AF.Identity, [("ww", pb), "dvc"], [("rp", pb)], bias=col(dv, DV_NEG1), scale=2.0)
                    ACT(uu[:, pb, :], uu[:, pb, :], AF.Sqrt, [("uu", pb)], [("uu", pb)])
                    STT(ip[:, pb, :], ip[:, pb, :], 1.0, cv[:, pb, :], ALU.add, ALU.mult, [("ip", pb), ("cv", pb)], [("ip", pb)])
                    P.op("pool", "tensor_tensor", dict(out=uu[:, pb, :], in0=uu[:, pb, :], in1=ww[:, pb, :], op=ALU.mult),
                         [("uu", pb), ("ww", pb)], [("uu", pb)])

                def stage_e3(k):
                    t, n = units[k]
                    pb = k % 2
                    TT(ip[:, pb, :], ip[:, pb, :], uu[:, pb, :], ALU.mult, [("uu", pb), ("ip", pb)], [("ip", pb)])
                    init = 0.0 if t == 0 else hcar[:, n:n + 1]
                    P.op("dve", "tensor_tensor_scan",
                         dict(out=cv[:, pb, :], data0=rp[:, pb, :], data1=ip[:, pb, :], initial=init, op0=ALU.mult, op1=ALU.add),
                         [("rp", pb), ("ip", pb), ("hcar", n)], [("cv", pb)])
                    if t + 1 < NT:
                        ACT(hcar[:, n:n + 1], cv[:, pb, 511:512], AF.Copy, [("cv", pb)], [("hcar", n)])
                    P.op("pool", "tensor_tensor", dict(out=yb[:, t % 2, n, :], in0=cv[:, pb, :], in1=gg[:, k % 3, :], op=ALU.mult),
                         [("cv", pb), ("gg", k % 3)], [("yb", t % 2, n)])
                    if n == NRC - 1:
                        for dc in range(KD):
                            b = bank_rr.next()
                            for m in range(NRC):
                                MM(banks[b][:, :], wo[:, m, dc * 128:(dc + 1) * 128], yb[:, t % 2, m, :], m == 0, m == NRC - 1,
                                   [("wo", m // 5), ("yb", t % 2, m)], [BK(b)], sig=(m == NRC - 1))
                            TT(xT[:, dc, TSL[t]], banks[b][:, :], xT[:, dc, TSL[t]], ALU.add, [], [BK(b), ("x", dc, t)])

                for k in range(min(3, NU)):
                    load_w(k)
                stage_a(0)
                for k in range(NU + 1):
                    if k + 3 < NU:
                        load_w(k + 3)
                    if k + 1 < NU:
                        stage_a(k + 1)
                    if k - 1 >= 0:
                        stage_e1(k - 1)
                    if k < NU:
                        stage_c1(k)
                    if k - 1 >= 0:
                        stage_e2(k - 1)
                    if k < NU:
                        stage_c2(k)
                    if k - 1 >= 0:
                        stage_e3(k - 1)
                P.barrier()

        def attention(gcol):
            with ExitStack() as fs:
                def fsb(name, shape, dt):
                    return fs.enter_context(nc.sbuf_tensor("a_" + name, list(shape), dt))
                sq = fsb("sq", [128, 4, 512], BF16)
                lnt = fsb("lnt", [128, 2, 512], F32)
                rs = fsb("rs", [128, 2, 512], F32)
                wqkv = fsb("wqkv", [128, 2, 3, KD, 128], BF16)
                wo = fsb("wo", [128, D], BF16)
                btb = fsb("btb", [128, 3, 2, 256], F32)
                qz = fsb("qz", [128, 2, 2, S], BF16)
                kT = fsb("kT", [128, 2, S], BF16)
                va = fsb("va", [128, 2, 16, 2, 128], BF16)
                oacc = fsb("oacc", [128, 2, S], F32)
                rec = fsb("rec", [128, 2, 512], F32)
                oT = fsb("oT", [128, S], BF16)
                st = fsb("st", [128, 3, 2, 256], F32)
                pt = fsb("pt", [128, 3, 2, 256], BF16)
                rmsnorm(gcol, (sq, lnt, rs))
                MEMSET(qz[:, :, :, :], 0.0, [("qk", 0, 0), ("qk", 0, 1)])
                for vb in range(2):
                    MEMSET(va[:, vb, :, 0, 64:128], 1.0, [("vaones", vb)])
                    MEMSET(va[:, vb, :, 1, 0:64], 1.0, [("vaones", vb)])
                wq_v = aqkv_d.rearrange("(k p) f -> p k f", p=128)
                sq_rr, ln_rr, st_rr = Ring(4), Ring(2), Ring(3)
                S_BANKS = Ring(2)
                PJ_BANKS = Ring(2)
                FIN_BANKS = Ring(2)
                SS_BANK, V_BANK = 6, 7
                units = [(p, g) for p in range(8) for g in range(3)]

                def load_unit(u):
                    p, g = units[u]
                    pg = u % 2
                    for qi in range(3):
                        c0 = (qi * 3 + g) * 1024 + p * 128
                        P.dma("pool", "wqkv%d_%d" % (pg, qi), wqkv[:, pg, qi, :, :], wq_v[:, :, c0:c0 + 128],
                              writes=[("wqkv", pg, qi)])
                    P.dma("sp", "bt%d" % (u % 3), btb[:, u % 3, :, :], bt_d[:, g * 16 + 2 * p:g * 16 + 2 * p + 2, :],
                          writes=[("bt", u % 3)])

                def proj_steps(u):
                    p, g = units[u]
                    pg = u % 2
                    d = DIL[g]
                    nb = (S // d) // 128
                    if u + 1 < len(units):
                        load_unit(u + 1)
                    tiles = [(qi, t) for qi in range(2) for t in range(NT)]
                    info = {}

                    def st_mm(j):
                        qi, t = tiles[j]
                        b = 4 + PJ_BANKS.next()
                        for k in range(KD):
                            MM(banks[b][:, :], wqkv[:, pg, qi, k, :], xn[:, k, TSL[t]], k == 0, k == KD - 1,
                               [("wqkv", pg, qi), ("xn", k, t)], [BK(b)], sig=(k == KD - 1))
                        info[j] = [b, None, None]

                    def st_sq(j):
                        b = info[j][0]
                        s = sq_rr.next()
                        ACT(sq[:, s, :], banks[b][:, :], AF.Square, [], [BK(b), ("sq", s)])
                        info[j][1] = s

                    def st_ss(j):
                        s = info[j][1]
                        MM(banks[SS_BANK][:, :], bd_bf[:, :], sq[:, s, :], True, True, [("sq", s), "bd"], [BK(SS_BANK)])

                    def st_le(j):
                        qi, t = tiles[j]
                        l = ln_rr.next()
                        ACT(lnt[:, l, :], banks[SS_BANK][:, :], AF.Ln, ["dvc"], [BK(SS_BANK), ("lnt", l)],
                            bias=col(dv, DV_EPS), scale=1.0 / 64)
                        ACT(rs[:, l, :], lnt[:, l, :], AF.Exp, [("lnt", l), "dvc"], [("rs", l)],
                            bias=(col(dv, DV_NLN8) if qi == 0 else None), scale=-0.5)
                        info[j][2] = l

                    def st_out(j):
                        qi, t = tiles[j]
                        b, s, l = info[j]
                        gain = col(pv, PV_QG) if qi == 0 else col(pv, PV_KG)
                        lc = 512 // d
                        if qi == 0:
                            parts = [(slice(0, 64), qz[0:64, pg, 0, :]), (slice(64, 128), qz[64:128, pg, 1, :])]
                        else:
                            parts = [(slice(0, 128), kT[:, pg, :])]
                        for ps_, dfull in parts:
                            if d == 1:
                                o_ap = dfull[:, TSL[t]]
                                i_ap = banks[b][ps_, :]
                                r_ap = rs[ps_, l, :]
                            else:
                                o_ap = dfull.rearrange("p (r l) -> p r l", r=d)[:, :, t * lc:(t + 1) * lc]
                                i_ap = banks[b][ps_, :].rearrange("p (l r) -> p r l", r=d)
                                r_ap = rs[ps_, l, :].rearrange("p (l r) -> p r l", r=d)
                            STT(o_ap, i_ap, gain[ps_, :], r_ap, ALU.mult, ALU.mult, [("rs", l), "pv"], [BK(b), ("qk", qi, pg)])

                    nt = len(tiles)
                    for j in range(nt + 1):
                        if 0 <= j - 1 < nt:
                            st_sq(j - 1)
                        yield
                        if 0 <= j - 1 < nt:
                            st_ss(j - 1)
                        if j < nt:
                            st_mm(j)
                        if 0 <= j - 1 < nt:
                            st_le(j - 1)
                            st_out(j - 1)
                        yield
                    for bq in range(4):
                        b = 4 + PJ_BANKS.next()
                        for bl in range(4):
                            B = 4 * bq + bl
                            r, jb = B // nb, B % nb
                            for k in range(KD):
                                if d == 1:
                                    lh = xn[:, k, B * 128:(B + 1) * 128]
                                else:
                                    lh = xn[:, k, :].rearrange("p (l r) -> p r l", r=d)[:, r, jb * 128:(jb + 1) * 128]
                                MM(banks[b][:, bl * 128:(bl + 1) * 128], lh, wqkv[:, pg, 2, k, :], k == 0, k == KD - 1,
                                   [("wqkv", pg, 2)] + [("xn", k, t) for t in range(NT)], [BK(b)],
                                   sig=(k == KD - 1 and bl == 3))
                        yield
                        for h in range(2):
                            ACT(va[:, pg, 4 * bq:4 * bq + 4, h, h * 64:(h + 1) * 64],
                                banks[b][:, :].rearrange("p (b c) -> p b c", b=4)[:, :, h * 64:(h + 1) * 64], AF.Copy,
                                [("vaones", pg)], [BK(b), ("va", pg, bq)])
                        yield

                NBLK = 16 * len(units)
                cinfo = {}

                def c_s(i):
                    u, B = i // 16, i % 16
                    p, g = units[u]
                    pg = u % 2
                    nb = (S // DIL[g]) // 128
                    jb = B % nb
                    sbk = 2 + (i % 2)
                    cs = slice(B * 128, (B + 1) * 128)
                    for h in range(2):
                        if jb > 0:
                            MM(banks[sbk][:, h * 256:h * 256 + 128], kT[:, pg, (B - 1) * 128:B * 128], qz[:, pg, h, cs],
                               True, True, [("qk", 0, pg), ("qk", 1, pg)], [BK(sbk)], sig=False)
                        MM(banks[sbk][:, h * 256 + 128:h * 256 + 256], kT[:, pg, cs], qz[:, pg, h, cs], True, True,
                           [("qk", 0, pg), ("qk", 1, pg)], [BK(sbk)], sig=(h == 1))

                def c_add(i):
                    u, B = i // 16, i % 16
                    p, g = units[u]
                    nb = (S // DIL[g]) // 128
                    lo = 0 if (B % nb) > 0 else 128
                    sbk = 2 + (i % 2)
                    si = i % 3
                    TT(st[:, si, :, lo:256], banks[sbk][:, :].rearrange("p (h c) -> p h c", h=2)[:, :, lo:256],
                       btb[:, u % 3, :, lo:256], ALU.add, [("bt", u % 3)], [BK(sbk), ("st", si)])

                def c_exp(i):
                    u, B = i // 16, i % 16
                    p, g = units[u]
                    nb = (S // DIL[g]) // 128
                    lo = 0 if (B % nb) > 0 else 128
                    si = i % 3
                    ACT(pt[:, si, :, lo:256], st[:, si, :, lo:256], AF.Exp, [("st", si)], [("pt", si)])

                def c_pv(i):
                    u, B = i // 16, i % 16
                    p, g = units[u]
                    pg = u % 2
                    d = DIL[g]
                    nb = (S // d) // 128
                    r, jb = B // nb, B % nb
                    si = i % 3
                    for h in range(2):
                        obk = h
                        oreg = banks[obk][:, (B % 4) * 128:(B % 4 + 1) * 128]
                        if jb > 0:
                            MM(oreg, va[:, pg, B - 1, h, :], pt[:, si, h, 0:128], True, False,
                               [("pt", si), ("va", pg, (B - 1) // 4), ("vaones", pg)], [BK(obk)], sig=False)
                        MM(oreg, va[:, pg, B, h, :], pt[:, si, h, 128:256], jb == 0, True,
                           [("pt", si), ("va", pg, B // 4), ("vaones", pg)], [BK(obk)])
                    if B % 4 == 3:
                        B0 = B - 3
                        for h in range(2):
                            obk = h
                            if d == 1:
                                o_ap = oacc[:, h, B0 * 128:(B0 + 4) * 128]
                                i_ap = banks[obk][:, :]
                            elif d == 4:
                                o_ap = oacc[:, h, :].rearrange("p (l r) -> p r l", r=4)[:, r, :]
                                i_ap = banks[obk][:, :]
                            else:
                                o_ap = oacc[:, h, :].rearrange("p (l r) -> p r l", r=16)[:, B0:B0 + 4, :]
                                i_ap = banks[obk][:, :].rearrange("p (r l) -> p r l", r=4)
                            if g == 0:
                                ACT(o_ap, i_ap, AF.Copy, [], [BK(obk), ("oacc", h)])
                            else:
                                TT(o_ap, i_ap, o_ap, ALU.add, [], [BK(obk), ("oacc", h)])

                def fin_steps(p):
                    P.dma("pool", "awo", wo[:, :], awo_d[p * 128:(p + 1) * 128, :], writes=[("awo",)])
                    for t in range(NT):
                        ts = TSL[t]
                        ri = t % 2
                        ACT(oacc[64:128, 0, ts], oacc[64:128, 0, ts], AF.Ln, [("oacc", 0)], [("oacc", 0)])
                        ACT(rec[0:64, ri, :], oacc[64:128, 0, ts], AF.Exp, [("oacc", 0)], [("rec", ri, 0)], scale=-1.0)
                        ACT(oacc[0:64, 1, ts], oacc[0:64, 1, ts], AF.Ln, [("oacc", 1)], [("oacc", 1)])
                        ACT(rec[64:128, ri, :], oacc[0:64, 1, ts], AF.Exp, [("oacc", 1)], [("rec", ri, 1)], scale=-1.0)
                        TT(oT[0:64, ts], oacc[0:64, 0, ts], rec[0:64, ri, :], ALU.mult, [("oacc", 0), ("rec", ri, 0)], [("oT", t, 0)])
                        TT(oT[64:128, ts], oacc[64:128, 1, ts], rec[64:128, ri, :], ALU.mult, [("oacc", 1), ("rec", ri, 1)], [("oT", t, 1)])
                        yield
                    for dc in range(KD):
                        for t in range(NT):
                            b = 7
                            MM(banks[b][:, :], wo[:, dc * 128:(dc + 1) * 128], oT[:, TSL[t]], True, True,
                               [("awo",), ("oT", t, 0), ("oT", t, 1)], [BK(b)])
                            TT(xT[:, dc, TSL[t]], banks[b][:, :], xT[:, dc, TSL[t]], ALU.add, [], [BK(b), ("x", dc, t)])
                            yield

                load_unit(0)
                for _ in proj_steps(0):
                    pass
                projg = None
                fing = []

                def proj_next():
                    nonlocal projg
                    if projg is not None:
                        try:
                            next(projg)
                        except StopIteration:
                            projg = None

                for i in range(NBLK + 3):
                    if i < NBLK and i % 16 == 1 and i // 16 + 1 < len(units):
                        projg = proj_steps(i // 16 + 1)
                    proj_next()
                    if 0 <= i - 1 < NBLK:
                        c_add(i - 1)
                    if 0 <= i - 2 < NBLK:
                        c_exp(i - 2)
                    if 0 <= i - 3 < NBLK:
                        c_pv(i - 3)
                        u3, B3 = (i - 3) // 16, (i - 3) % 16
                        if B3 == 15 and units[u3][1] == 2:
                            fing.append((fin_steps(units[u3][0]), [0]))
                    if i < NBLK:
                        c_s(i)
                    proj_next()
                    if fing:
                        gen, cnt = fing[0]
                        for _ in range(1 if cnt[0] < NT else 3):
                            try:
                                next(gen)
                                cnt[0] += 1
                            except StopIteration:
                                fing.pop(0)
                                break
                for fg, _cnt in fing:
                    for _ in fg:
                        pass
                P.barrier()

        P.barrier()
        for ph in PHASES:
            if ph == "ffn0":
                ffn(0, PV_G + 0 * 8)
            elif ph == "rnn":
                rnn(PV_G + 1 * 8)
            elif ph == "ffn1":
                ffn(1, PV_G + 2 * 8)
            elif ph == "ffn2":
                ffn(2, PV_G + 3 * 8)
            elif ph == "att":
                attention(PV_G + 4 * 8)
            elif ph == "ffn3":
                ffn(3, PV_G + 5 * 8)

        out_dv = out_d.rearrange("(k p) t -> p k t", p=128)
        for k in range(KD):
            P.dma("sp", "xout", out_dv[:, k, :], xT[:, k, :], reads=[("x", k, t) for t in range(NT)])
        P.final_wait("sp", "xout")

        with nc.Block() as block:
            @block.tensor
            def _(e):
                P.emit("pe", e)

            @block.scalar
            def _(e):
                P.emit("act", e)

            @block.vector
            def _(e):
                P.emit("dve", e)

            @block.gpsimd
            def _(e):
                P.emit("pool", e)

            @block.sync
            def _(e):
                P.emit("sp", e)
    return nc


PHASES = ("ffn0", "rnn", "ffn1", "ffn2", "att", "ffn3")
DBG = {}


def _bucket_table():
    k = np.arange(128)[:, None]
    j = np.arange(256)[None, :]
    dist = np.where(j < 128, j + 128 - k, (j - 128) - k)
    valid = (dist >= 0) & (dist <= 128)
    tabs = []
    for d in DIL:
        n = np.maximum(dist * d, 0).astype(np.int32)
        nf = np.maximum(n, 1).astype(np.float32)
        large = 16 + (np.log(nf / np.float32(16)) / np.float32(math.log(2048 / 16)) * np.float32(16)).astype(np.int32)
        large = np.minimum(large, 31)
        bk = np.where(n < 16, n, large)
        tabs.append(np.where(valid, bk, 32))
    return np.stack(tabs, 0)


_CACHE = {}


def kernel(x, norm_g, ffn_w_in, ffn_w_out, rnn_w_in, rnn_conv_w, rnn_conv_b, rnn_w_a, rnn_b_a,
           rnn_w_x, rnn_b_x, rnn_lambda, rnn_w_out, att_w_qkv, att_q_gain, att_k_gain, att_w_o, rel_bias):
    f = lambda a: np.ascontiguousarray(np.asarray(a, dtype=np.float32))
    x = f(x)
    pv = np.zeros((128, PV_N), np.float32)
    pv[:, PV_G:PV_G + 48] = f(norm_g).reshape(6, KD, 128).transpose(2, 0, 1).reshape(128, 48)
    pv[:, PV_CW:PV_CW + 40] = f(rnn_conv_w)[0].reshape(4, NRC, 128).transpose(2, 1, 0).reshape(128, 40)
    pv[:, PV_CB:PV_CB + 10] = f(rnn_conv_b)[0].reshape(NRC, 128).T
    pv[:, PV_BA:PV_BA + 10] = f(rnn_b_a)[0].reshape(NRC, 128).T
    pv[:, PV_BX:PV_BX + 10] = f(rnn_b_x)[0].reshape(NRC, 128).T
    pv[:, PV_LAM:PV_LAM + 10] = f(rnn_lambda)[0].reshape(NRC, 128).T
    pv[:, PV_QG] = np.tile(f(att_q_gain)[0], 2)
    pv[:, PV_KG] = np.tile(f(att_k_gain)[0], 2)
    rb = np.concatenate([f(rel_bias), np.full((1, 48), NEG, np.float32)], 0)
    bk = _bucket_table()
    bt = np.empty((128, 48, 256), np.float32)
    for g in range(3):
        bt[:, g * 16:(g + 1) * 16, :] = rb[bk[g]][:, :, g * 16:(g + 1) * 16].transpose(0, 2, 1)
    shared = {
        "pv": pv, "bt": bt,
        "ffn_w_in": f(ffn_w_in).reshape(4 * D, 2 * DFF),
        "ffn_w_out": f(ffn_w_out).reshape(4 * DFF, D),
        "rnn_w_in": f(rnn_w_in).reshape(D, 2 * DRNN),
        "rnn_w_a": f(rnn_w_a).reshape(DRNN, 128),
        "rnn_w_x": f(rnn_w_x).reshape(DRNN, 128),
        "rnn_w_out": f(rnn_w_out).reshape(DRNN, D),
        "att_w_qkv": f(att_w_qkv).reshape(D, 9216),
        "att_w_o": f(att_w_o).reshape(D, D),
    }
    in_maps = []
    for c in range(N_CORES):
        m = dict(shared)
        m["xT"] = np.ascontiguousarray(x[c].T)
        in_maps.append(m)
    if "nc" not in _CACHE:
        _CACHE["nc"] = build_program()
    res = run_bass_kernel_spmd(_CACHE["nc"], in_maps, core_ids=list(range(N_CORES)))
    out = np.stack([np.ascontiguousarray(res.results[c]["outT"].T) for c in range(N_CORES)], 0)
    return out.astype(np.float32)
```

```python
import math
from contextlib import ExitStack

import numpy as np
import concourse.bass as bass
import concourse.mybir as mybir
from concourse.bass_utils import run_bass_kernel_spmd

F32 = mybir.dt.float32
BF16 = mybir.dt.bfloat16
AF = mybir.ActivationFunctionType
ALU = mybir.AluOpType

D = 1024
S = 2048
KD = D // 128
NT = S // 512
DFF = 2816
NFC = DFF // 128
DRNN = 1280
NRC = DRNN // 128
EPS = 1e-6
NEG = -30000.0
DIL = (1, 4, 16)
N_CORES = 8

PV_G = 0
PV_CW = 48
PV_CB = 88
PV_BA = 98
PV_BX = 108
PV_LAM = 118
PV_QG = 128
PV_KG = 129
PV_N = 130


class Prog:
    ENG = ("pe", "act", "dve", "pool", "sp")
    CENG = ("pe", "act", "dve", "pool")

    def __init__(self, nc, es):
        self.nc = nc
        self.es = es
        self.q = {e: [] for e in self.ENG}
        self.seq = {e: 0 for e in self.ENG}
        self.waited = {e: {} for e in self.ENG}
        self.lastw = {}
        self.readers = {}
        self.sems = {}
        self.dcount = {}
        self.needed = {e: set() for e in self.CENG}
        for e in self.CENG:
            self.sems[e] = es.enter_context(nc.semaphore("s_" + e))

    def _sem(self, sk):
        if sk not in self.sems:
            self.sems[sk] = self.es.enter_context(self.nc.semaphore("d_" + sk))
        return self.sems[sk]

    def _wait(self, eng, sk, v):
        if self.waited[eng].get(sk, 0) >= v:
            return
        self.waited[eng][sk] = v
        self.q[eng].append(("wait", sk, v))
        if sk in self.needed:
            self.needed[sk].add(v)

    def _deps(self, eng, reads, writes):
        deps = {}

        def add(d):
            if d is None:
                return
            sk, v = d
            if deps.get(sk, 0) < v:
                deps[sk] = v

        for k in reads:
            add(self.lastw.get(k))
        for k in writes:
            add(self.lastw.get(k))
            for sk, v in self.readers.get(k, {}).items():
                add((sk, v))
        for sk, v in deps.items():
            if sk == eng and eng == "pe":
                continue
            self._wait(eng, sk, v)

    def _mark(self, sk, val, reads, writes):
        for k in writes:
            self.lastw[k] = (sk, val)
            self.readers[k] = {}
        for k in reads:
            r = self.readers.setdefault(k, {})
            if r.get(sk, 0) < val:
                r[sk] = val

    def op(self, eng, name, kw, reads=(), writes=(), sig=True):
        self._deps(eng, reads, writes)
        if sig:
            self.seq[eng] += 1
            oid = self.seq[eng]
        else:
            oid = None
        self.q[eng].append(("op", (name, kw), oid, sig))
        self._mark(eng, self.seq[eng] if sig else self.seq[eng] + 1, reads, writes)

    def barrier(self):
        for eng in self.ENG:
            for f in self.CENG:
                if f == eng or self.seq[f] == 0:
                    continue
                self._wait(eng, f, self.seq[f])

    def dma(self, eng, stream, out, in_, reads=(), writes=()):
        self._sem(stream)
        self._deps(eng, reads, writes)
        self.dcount[stream] = self.dcount.get(stream, 0) + 16
        self.q[eng].append(("dma", out, in_, stream))
        self._mark(stream, self.dcount[stream], reads, writes)

    def final_wait(self, eng, stream):
        self.q[eng].append(("wait", stream, self.dcount[stream]))

    def _rank(self, eng):
        ids = sorted(self.needed[eng])
        return {v: i + 1 for i, v in enumerate(ids)}

    def emit(self, eng, e):
        my = self.sems.get(eng)
        ranks = {f: self._rank(f) for f in self.CENG}
        myneed = self.needed.get(eng, set())
        for item in self.q[eng]:
            if item[0] == "wait":
                sk, v = item[1], item[2]
                if sk in ranks:
                    v = ranks[sk][v]
                e.wait_ge(self._sem(sk), v)
            elif item[0] == "op":
                name, kw = item[1]
                ins = getattr(e, name)(**kw)
                if item[2] is not None and item[2] in myneed:
                    ins.then_inc(my, 1)
            else:
                e.dma_start(out=item[1], in_=item[2]).then_inc(self._sem(item[3]), 16)


class Ring:
    def __init__(self, n):
        self.n = n
        self.i = -1

    def next(self):
        self.i = (self.i + 1) % self.n
        return self.i


def build_program():
    nc = bass.Bass("TRN2", target_bir_lowering=False)

    def din(name, shape):
        return nc.dram_tensor(name, list(shape), F32, kind="ExternalInput").ap()

    xT_d = din("xT", [D, S])
    pv_d = din("pv", [128, PV_N])
    bt_d = din("bt", [128, 48, 256])
    fwi_d = din("ffn_w_in", [4 * D, 2 * DFF])
    fwo_d = din("ffn_w_out", [4 * DFF, D])
    rwi_d = din("rnn_w_in", [D, 2 * DRNN])
    rwa_d = din("rnn_w_a", [DRNN, 128])
    rwx_d = din("rnn_w_x", [DRNN, 128])
    rwo_d = din("rnn_w_out", [DRNN, D])
    aqkv_d = din("att_w_qkv", [D, 9216])
    awo_d = din("att_w_o", [D, D])
    out_d = nc.dram_tensor("outT", [D, S], F32, kind="ExternalOutput").ap()

    with ExitStack() as es:
        P = Prog(nc, es)

        def sb(name, shape, dt):
            return es.enter_context(nc.sbuf_tensor(name, list(shape), dt))

        xT = sb("xT_sb", [128, KD, S], F32)
        xn = sb("xn_sb", [128, KD, S], BF16)
        pv = sb("pv_sb", [128, PV_N], F32)
        dv = sb("dv_sb", [128, 64], F32)
        ones_bf = sb("ones_bf", [128, 128], BF16)
        bd_bf = sb("bd_bf", [128, 128], BF16)
        banks = [es.enter_context(nc.psum_tensor("bank%d" % i, [128, 512], F32)) for i in range(8)]
        DV_EPS, DV_ONE, DV_NLN8, DV_NEG1, DV_HBA, DV_HBX, DV_C4, DV_TMP = 0, 1, 2, 3, 4, 14, 24, 34

        bank_rr = Ring(8)

        def BK(i):
            return ("bank", i)

        def MM(out, lhsT, rhs, start, stop, reads, writes, sig=True):
            P.op("pe", "matmul", dict(out=out, lhsT=lhsT, rhs=rhs, start=start, stop=stop), reads, writes, sig)

        def ACT(out, in_, func, reads, writes, bias=None, scale=None):
            kw = dict(out=out, in_=in_, func=func)
            if bias is not None:
                kw["bias"] = bias
            if scale is not None:
                kw["scale"] = scale
            P.op("act", "activation", kw, reads, writes)

        def TS(out, in0, s1, s2, op0, op1, reads, writes):
            kw = dict(out=out, in0=in0, scalar1=s1, scalar2=s2, op0=op0)
            if op1 is not None:
                kw["op1"] = op1
            P.op("dve", "tensor_scalar", kw, reads, writes)

        def STT(out, in0, scalar, in1, op0, op1, reads, writes):
            P.op("dve", "scalar_tensor_tensor", dict(out=out, in0=in0, scalar=scalar, in1=in1, op0=op0, op1=op1),
                 reads, writes)

        def TT(out, in0, in1, op, reads, writes):
            P.op("dve", "tensor_tensor", dict(out=out, in0=in0, in1=in1, op=op), reads, writes)

        def RECIP(out, in_, reads, writes):
            P.op("dve", "reciprocal", dict(out=out, in_=in_), reads, writes)

        def RECIP_FAST(out, in_, reads, writes):
            P.op("dve", "reciprocal_approx_fast", dict(out=out, in_=in_), reads, writes)

        def MEMSET(ap, val, writes):
            P.op("dve", "memset", dict(ap=ap, constant=val), (), writes)

        def col(t, c, n=1):
            return t[:, c:c + n]

        xT_dv = xT_d.rearrange("(k p) t -> p k t", p=128)
        for k in range(KD):
            P.dma("sp", "xin%d" % k, xT[:, k, :], xT_dv[:, k, :],
                  writes=[("x", k, t) for t in range(NT)])
        P.dma("sp", "pvin", pv[:], pv_d, writes=["pv"])

        MEMSET(ones_bf[:], 1.0, ["ones"])
        MEMSET(bd_bf[:], 0.0, ["bd"])
        MEMSET(bd_bf[0:64, 0:64], 1.0, ["bd"])
        MEMSET(bd_bf[64:128, 64:128], 1.0, ["bd"])
        MEMSET(col(dv, DV_EPS), EPS, ["dvc"])
        MEMSET(col(dv, DV_ONE), 1.0, ["dvc"])
        MEMSET(col(dv, DV_NLN8), -math.log(8.0), ["dvc"])
        MEMSET(col(dv, DV_NEG1), -1.0, ["dvc"])
        TS(col(dv, DV_HBA, 20), col(pv, PV_BA, 20), 0.5, None, ALU.mult, None, ["pv"], ["dvh"])
        ACT(col(dv, DV_TMP, 10), col(pv, PV_LAM, 10), AF.Exp, ["pv"], ["dvt"], scale=-1.0)
        ACT(col(dv, DV_TMP + 10, 10), col(dv, DV_TMP, 10), AF.Ln, ["dvt", "dvc"], ["dvt2"], bias=col(dv, DV_ONE), scale=1.0)
        TS(col(dv, DV_C4, 10), col(dv, DV_TMP + 10, 10), 2.0, None, ALU.mult, None, ["dvt2"], ["dvc4"])

        TSL = [slice(t * 512, (t + 1) * 512) for t in range(NT)]

        def rmsnorm(gcol, tmp):
            sq, lnt, rs = tmp
            sq_rr, ln_rr = Ring(4), Ring(2)
            for t in range(NT):
                ts = TSL[t]
                b = bank_rr.next()
                for k in range(KD):
                    s = sq_rr.next()
                    ACT(sq[:, s, :], xT[:, k, ts], AF.Square, [("x", k, t)], [("sq", s)])
                    MM(banks[b][:, :], ones_bf[:, :], sq[:, s, :], k == 0, k == KD - 1,
                       [("sq", s), "ones"], [BK(b)])
                l = ln_rr.next()
                ACT(lnt[:, l, :], banks[b][:, :], AF.Ln, ["dvc"], [BK(b), ("lnt", l)], bias=col(dv, DV_EPS), scale=1.0 / D)
                ACT(rs[:, l, :], lnt[:, l, :], AF.Exp, [("lnt", l)], [("rs", l)], scale=-0.5)
                for k in range(KD):
                    STT(xn[:, k, ts], xT[:, k, ts], col(pv, gcol + k), rs[:, l, :], ALU.mult, ALU.mult,
                        [("x", k, t), ("rs", l), "pv"], [("xn", k, t)])

        def ffn(idx, gcol):
            with ExitStack() as fs:
                def fsb(name, shape, dt):
                    return fs.enter_context(nc.sbuf_tensor("%s_%d" % (name, idx), list(shape), dt))
                NSL = 2
                CPS = NFC // NSL
                gT = fsb("gT", [128, CPS, S], BF16)
                win = fsb("win", [128, 4, 2, KD, 128], BF16)
                wout = fsb("wout", [128, 4, CPS, 128], BF16)
                sq = fsb("sq", [128, 4, 512], BF16)
                lnt = fsb("lnt", [128, 2, 512], F32)
                rs = fsb("rs", [128, 2, 512], F32)
                sg = fsb("sg", [128, 3, 512], F32)
                rmsnorm(gcol, (sq, lnt, rs))
                if DBG.get("ffn_stop") == "norm":
                    P.barrier()
                    return
                wi_v = fwi_d[idx * D:(idx + 1) * D, :].rearrange("(k p) f -> p k f", p=128)
                wo_v = fwo_d[idx * DFF:(idx + 1) * DFF, :].rearrange("(c p) d -> p c d", p=128)
                win_rr, wout_rr, sg_rr = Ring(4), Ring(4), Ring(3)
                for sl in range(NSL):
                    for ci in range(CPS):
                        c = sl * CPS + ci
                        ws = win_rr.next()
                        for gu in range(2):
                            c0 = gu * DFF + c * 128
                            P.dma("pool", "win%d_%d" % (ws, gu), win[:, ws, gu, :, :], wi_v[:, :, c0:c0 + 128],
                                  writes=[("win", ws, gu)])
                        for t in range(NT):
                            ts = TSL[t]
                            bg = bank_rr.next()
                            bu = bank_rr.next()
                            for gu, b in ((0, bg), (1, bu)):
                                for k in range(KD):
                                    MM(banks[b][:, :], win[:, ws, gu, k, :], xn[:, k, ts], k == 0, k == KD - 1,
                                       [("win", ws, gu), ("xn", k, t)], [BK(b)], sig=(k == KD - 1))
                            s = sg_rr.next()
                            ACT(sg[:, s, :], banks[bg][:, :], AF.Silu, [], [BK(bg), ("sg", s)])
                            TT(gT[:, ci, ts], sg[:, s, :], banks[bu][:, :], ALU.mult, [("sg", s)], [BK(bu), ("gT", ci, t)])
                    if DBG.get("ffn_stop") == "up":
                        continue
                    for dc in range(KD):
                        ws = wout_rr.next()
                        for hf, (c_lo, c_hi) in enumerate(((0, 6), (6, CPS))):
                            P.dma("pool", "wout%d_%d" % (ws, hf), wout[:, ws, c_lo:c_hi, :],
                                  wo_v[:, sl * CPS + c_lo:sl * CPS + c_hi, dc * 128:(dc + 1) * 128],
                                  writes=[("wout", ws, hf)])
                        for t in range(NT):
                            ts = TSL[t]
                            b = bank_rr.next()
                            for ci in range(CPS):
                                MM(banks[b][:, :], wout[:, ws, ci, :], gT[:, ci, ts], ci == 0, ci == CPS - 1,
                                   [("wout", ws, 0 if ci < 6 else 1), ("gT", ci, t)], [BK(b)], sig=(ci == CPS - 1))
                            STT(xT[:, dc, ts], banks[b][:, :], 0.5, xT[:, dc, ts], ALU.mult, ALU.add,
                                [], [BK(b), ("x", dc, t)])
                P.barrier()

        def rnn(gcol):
            with ExitStack() as fs:
                def fsb(name, shape, dt):
                    return fs.enter_context(nc.sbuf_tensor("r_" + name, list(shape), dt))
                sq = fsb("sq", [128, 4, 512], BF16)
                lnt = fsb("lnt", [128, 2, 512], F32)
                rs = fsb("rs", [128, 2, 512], F32)
                wgu = fsb("wgu", [128, 3, 2, KD, 128], BF16)
                wax = fsb("wax", [128, NRC, 2, 128], BF16)
                wo = fsb("wo", [128, NRC, D], BF16)
                gg = fsb("gg", [128, 3, 512], F32)
                ub = fsb("ub", [128, 2, 516], F32)
                cv = fsb("cv", [128, 2, 512], F32)
                cvb = fsb("cvb", [128, 2, 512], BF16)
                rp = fsb("rp", [128, 2, 512], F32)
                ip = fsb("ip", [128, 2, 512], F32)
                uu = fsb("uu", [128, 2, 512], F32)
                ww = fsb("ww", [128, 2, 512], F32)
                yb = fsb("yb", [128, 2, NRC, 512], BF16)
                ucar = fsb("ucar", [128, NRC, 4], F32)
                hcar = fsb("hcar", [128, NRC], F32)
                rmsnorm(gcol, (sq, lnt, rs))
                MEMSET(ucar[:, :, :], 0.0, [("ucar", n) for n in range(NRC)])
                wa_v = rwa_d.rearrange("(n p) d -> p n d", p=128)
                wx_v = rwx_d.rearrange("(n p) d -> p n d", p=128)
                wo_v = rwo_d.rearrange("(n p) d -> p n d", p=128)
                for hf in range(2):
                    ns = slice(hf * 5, hf * 5 + 5)
                    P.dma("pool", "rwa%d" % hf, wax[:, ns, 0, :], wa_v[:, ns, :], writes=[("wax", 0, hf)])
                    P.dma("pool", "rwx%d" % hf, wax[:, ns, 1, :], wx_v[:, ns, :], writes=[("wax", 1, hf)])
                    P.dma("pool", "rwo%d" % hf, wo[:, ns, :], wo_v[:, ns, :], writes=[("wo", hf)])
                wi_v = rwi_d.rearrange("(k p) f -> p k f", p=128)
                NU = NT * NRC
                units = [(t, n) for t in range(NT) for n in range(NRC)]

                def load_w(k):
                    t, n = units[k]
                    ws = k % 3
                    for gu in range(2):
                        c0 = gu * DRNN + n * 128
                        P.dma("pool", "rwgu%d_%d" % (ws, gu), wgu[:, ws, gu, :, :], wi_v[:, :, c0:c0 + 128],
                              writes=[("wgu", ws, gu)])

                def stage_a(k):
                    t, n = units[k]
                    pb, ws, ts = k % 2, k % 3, TSL[t]
                    b = bank_rr.next()
                    for kd in range(KD):
                        MM(banks[b][:, :], wgu[:, ws, 0, kd, :], xn[:, kd, ts], kd == 0, kd == KD - 1,
                           [("wgu", ws, 0), ("xn", kd, t)], [BK(b)], sig=(kd == KD - 1))
                    ACT(gg[:, k % 3, :], banks[b][:, :], AF.Gelu_apprx_tanh, [], [BK(b), ("gg", k % 3)])
                    b = bank_rr.next()
                    for kd in range(KD):
                        MM(banks[b][:, :], wgu[:, ws, 1, kd, :], xn[:, kd, ts], kd == 0, kd == KD - 1,
                           [("wgu", ws, 1), ("xn", kd, t)], [BK(b)], sig=(kd == KD - 1))
                    ACT(ub[:, pb, 3:515], banks[b][:, :], AF.Copy, [], [BK(b), ("ub", pb)])
                    ACT(ub[:, pb, 0:3], ucar[:, n, 0:3], AF.Copy, [("ucar", n)], [("ub", pb)])
                    ACT(ucar[:, n, 0:3], ub[:, pb, 512:515], AF.Copy, [("ub", pb)], [("ucar", n)])

                def stage_c1(k):
                    t, n = units[k]
                    pb = k % 2
                    cw = PV_CW + n * 4
                    TS(cv[:, pb, :], ub[:, pb, 0:512], col(pv, cw), col(pv, PV_CB + n), ALU.mult, ALU.add,
                       [("ub", pb), "pv"], [("cv", pb)])
                    for kk in range(1, 4):
                        STT(cv[:, pb, :], ub[:, pb, kk:kk + 512], col(pv, cw + kk), cv[:, pb, :], ALU.mult, ALU.add,
                            [("ub", pb), "pv"], [("cv", pb)])

                def stage_c2(k):
                    t, n = units[k]
                    pb = k % 2
                    ACT(cvb[:, pb, :], cv[:, pb, :], AF.Copy, [("cv", pb)], [("cvb", pb)])
                    br = bank_rr.next()
                    MM(banks[br][:, :], wax[:, n, 0, :], cvb[:, pb, :], True, True, [("wax", 0, n // 5), ("cvb", pb)], [BK(br)])
                    bi = bank_rr.next()
                    MM(banks[bi][:, :], wax[:, n, 1, :], cvb[:, pb, :], True, True, [("wax", 1, n // 5), ("cvb", pb)], [BK(bi)])
                    ACT(rp[:, pb, :], banks[br][:, :], AF.Tanh, ["dvh"], [BK(br), ("rp", pb)], bias=col(dv, DV_HBA + n), scale=0.5)
                    ACT(ip[:, pb, :], banks[bi][:, :], AF.Tanh, ["dvh"], [BK(bi), ("ip", pb)], bias=col(dv, DV_HBX + n), scale=0.5)
                    ACT(uu[:, pb, :], rp[:, pb, :], AF.Tanh, [("rp", pb), "dvc4"], [("uu", pb)],
                        bias=col(dv, DV_C4 + n), scale=col(dv, DV_C4 + n))
                    ACT(ww[:, pb, :], uu[:, pb, :], AF.Identity, [("uu", pb), "dvc"], [("ww", pb)], bias=col(dv, DV_ONE), scale=1.0)

                def stage_e1(k):
                    pb = k % 2
                    RECIP(ww[:, pb, :], ww[:, pb, :], [("ww", pb)], [("ww", pb)])

                def stage_e2(k):
                    t, n = units[k]
                    pb = k % 2
                    ACT(rp[:, pb, :], ww[:, pb, :], AF.Identity, [("ww", pb), "dvc"], [("rp", pb)], bias=col(dv, DV_NEG1), scale=2.0)
                    ACT(uu[:, pb, :], uu[:, pb, :], AF.Sqrt, [("uu", pb)], [("uu", pb)])
                    STT(ip[:, pb, :], ip[:, pb, :], 1.0, cv[:, pb, :], ALU.add, ALU.mult, [("ip", pb), ("cv", pb)], [("ip", pb)])
                    P.op("pool", "tensor_tensor", dict(out=uu[:, pb, :], in0=uu[:, pb, :], in1=ww[:, pb, :], op=ALU.mult),
                         [("uu", pb), ("ww", pb)], [("uu", pb)])

                def stage_e3(k):
                    t, n = units[k]
                    pb = k % 2
                    TT(ip[:, pb, :], ip[:, pb, :], uu[:, pb, :], ALU.mult, [("uu", pb), ("ip", pb)], [("ip", pb)])
                    init = 0.0 if t == 0 else hcar[:, n:n + 1]
                    P.op("dve", "tensor_tensor_scan",
                         dict(out=cv[:, pb, :], data0=rp[:, pb, :], data1=ip[:, pb, :], initial=init, op0=ALU.mult, op1=ALU.add),
                         [("rp", pb), ("ip", pb), ("hcar", n)], [("cv", pb)])
                    if t + 1 < NT:
                        ACT(hcar[:, n:n + 1], cv[:, pb, 511:512], AF.Copy, [("cv", pb)], [("hcar", n)])
                    P.op("pool", "tensor_tensor", dict(out=yb[:, t % 2, n, :], in0=cv[:, pb, :], in1=gg[:, k % 3, :], op=ALU.mult),
                         [("cv", pb), ("gg", k % 3)], [("yb", t % 2, n)])
                    if n == NRC - 1:
                        for dc in range(KD):
                            b = bank_rr.next()
                            for m in range(NRC):
                                MM(banks[b][:, :], wo[:, m, dc * 128:(dc + 1) * 128], yb[:, t % 2, m, :], m == 0, m == NRC - 1,
                                   [("wo", m // 5), ("yb", t % 2, m)], [BK(b)], sig=(m == NRC - 1))
                            TT(xT[:, dc, TSL[t]], banks[b][:, :], xT[:, dc, TSL[t]], ALU.add, [], [BK(b), ("x", dc, t)])

                for k in range(min(3, NU)):
                    load_w(k)
                stage_a(0)
                for k in range(NU + 1):
                    if k + 3 < NU:
                        load_w(k + 3)
                    if k + 1 < NU:
                        stage_a(k + 1)
                    if k - 1 >= 0:
                        stage_e1(k - 1)
                    if k < NU:
                        stage_c1(k)
                    if k - 1 >= 0:
                        stage_e2(k - 1)
                    if k < NU:
                        stage_c2(k)
                    if k - 1 >= 0:
                        stage_e3(k - 1)
                P.barrier()

        def attention(gcol):
            with ExitStack() as fs:
                def fsb(name, shape, dt):
                    return fs.enter_context(nc.sbuf_tensor("a_" + name, list(shape), dt))
                sq = fsb("sq", [128, 4, 512], BF16)
                lnt = fsb("lnt", [128, 2, 512], F32)
                rs = fsb("rs", [128, 2, 512], F32)
                wqkv = fsb("wqkv", [128, 2, 3, KD, 128], BF16)
                wo = fsb("wo", [128, D], BF16)
                btb = fsb("btb", [128, 3, 2, 256], F32)
                qz = fsb("qz", [128, 2, 2, S], BF16)
                qs = fsb("qs", [128, S], BF16)
                kT = fsb("kT", [128, 2, S], BF16)
                va = fsb("va", [128, 2, 16, 2, 128], BF16)
                oacc = fsb("oacc", [128, 2, S], F32)
                rec = fsb("rec", [128, 2, 512], F32)
                oT = fsb("oT", [128, S], BF16)
                st = fsb("st", [128, 3, 2, 256], F32)
                pt = fsb("pt", [128, 3, 2, 256], BF16)
                rmsnorm(gcol, (sq, lnt, rs))
                MEMSET(qz[:, :, :, :], 0.0, [("qk", 0, 0), ("qk", 0, 1)])
                for vb in range(2):
                    MEMSET(va[:, vb, :, 0, 64:128], 1.0, [("vaones", vb)])
                    MEMSET(va[:, vb, :, 1, 0:64], 1.0, [("vaones", vb)])
                wq_v = aqkv_d.rearrange("(k p) f -> p k f", p=128)
                sq_rr, ln_rr, st_rr = Ring(4), Ring(2), Ring(3)
                S_BANKS = Ring(2)
                PJ_BANKS = Ring(2)
                FIN_BANKS = Ring(2)
                SS_BANK, V_BANK = 6, 7
                units = [(p, g) for p in range(8) for g in range(3)]

                def load_unit(u):
                    p, g = units[u]
                    pg = u % 2
                    for qi in range(3):
                        c0 = (qi * 3 + g) * 1024 + p * 128
                        P.dma("pool", "wqkv%d_%d" % (pg, qi), wqkv[:, pg, qi, :, :], wq_v[:, :, c0:c0 + 128],
                              writes=[("wqkv", pg, qi)])
                    P.dma("sp", "bt%d" % (u % 3), btb[:, u % 3, :, :], bt_d[:, g * 16 + 2 * p:g * 16 + 2 * p + 2, :],
                          writes=[("bt", u % 3)])

                def proj_steps(u):
                    p, g = units[u]
                    pg = u % 2
                    d = DIL[g]
                    nb = (S // d) // 128
                    if u + 1 < len(units):
                        load_unit(u + 1)
                    tiles = [(qi, t) for qi in range(2) for t in range(NT)]
                    info = {}

                    def st_mm(j):
                        qi, t = tiles[j]
                        b = 4 + PJ_BANKS.next()
                        for k in range(KD):
                            MM(banks[b][:, :], wqkv[:, pg, qi, k, :], xn[:, k, TSL[t]], k == 0, k == KD - 1,
                               [("wqkv", pg, qi), ("xn", k, t)], [BK(b)], sig=(k == KD - 1))
                        info[j] = [b, None, None]

                    def st_sq(j):
                        b = info[j][0]
                        s = sq_rr.next()
                        ACT(sq[:, s, :], banks[b][:, :], AF.Square, [], [BK(b), ("sq", s)])
                        info[j][1] = s

                    def st_ss(j):
                        s = info[j][1]
                        MM(banks[SS_BANK][:, :], bd_bf[:, :], sq[:, s, :], True, True, [("sq", s), "bd"], [BK(SS_BANK)])

                    def st_le(j):
                        qi, t = tiles[j]
                        l = ln_rr.next()
                        ACT(lnt[:, l, :], banks[SS_BANK][:, :], AF.Ln, ["dvc"], [BK(SS_BANK), ("lnt", l)],
                            bias=col(dv, DV_EPS), scale=1.0 / 64)
                        ACT(rs[:, l, :], lnt[:, l, :], AF.Exp, [("lnt", l), "dvc"], [("rs", l)],
                            bias=(col(dv, DV_NLN8) if qi == 0 else None), scale=-0.5)
                        info[j][2] = l

                    def st_out(j):
                        qi, t = tiles[j]
                        b, s, l = info[j]
                        gain = col(pv, PV_QG) if qi == 0 else col(pv, PV_KG)
                        lc = 512 // d
                        if qi == 0:
                            parts = [(slice(0, 64), qz[0:64, pg, 0, :]), (slice(64, 128), qz[64:128, pg, 1, :])]
                        else:
                            parts = [(slice(0, 128), kT[:, pg, :])]
                        def region(full):
                            if d == 1:
                                return full[:, TSL[t]]
                            return full.rearrange("p (r l) -> p r l", r=d)[:, :, t * lc:(t + 1) * lc]

                        if qi == 0:
                            if d == 1:
                                i_ap, r_ap = banks[b][:, :], rs[:, l, :]
                            else:
                                i_ap = banks[b][:, :].rearrange("p (l r) -> p r l", r=d)
                                r_ap = rs[:, l, :].rearrange("p (l r) -> p r l", r=d)
                            qkeys = [("qs", t % 2)] + ([("qs", 1)] if t == 0 else [])
                            STT(region(qs[:, :]), i_ap, gain, r_ap, ALU.mult, ALU.mult, [("rs", l), "pv"], [BK(b)] + qkeys)
                            P.op("pool", "tensor_copy", dict(out=region(qz[0:64, pg, 0, :]), in_=region(qs[0:64, :])),
                                 [("qs", t % 2)], [("qk", 0, pg)])
                            P.op("pool", "tensor_copy", dict(out=region(qz[64:128, pg, 1, :]), in_=region(qs[64:128, :])),
                                 [("qs", t % 2)], [("qk", 0, pg)])
                            return
                        for ps_, dfull in parts:
                            if d == 1:
                                o_ap = dfull[:, TSL[t]]
                                i_ap = banks[b][ps_, :]
                                r_ap = rs[ps_, l, :]
                            else:
                                o_ap = dfull.rearrange("p (r l) -> p r l", r=d)[:, :, t * lc:(t + 1) * lc]
                                i_ap = banks[b][ps_, :].rearrange("p (l r) -> p r l", r=d)
                                r_ap = rs[ps_, l, :].rearrange("p (l r) -> p r l", r=d)
                            STT(o_ap, i_ap, gain[ps_, :], r_ap, ALU.mult, ALU.mult, [("rs", l), "pv"], [BK(b), ("qk", qi, pg)])

                    nt = len(tiles)
                    for j in range(nt + 1):
                        if 0 <= j - 1 < nt:
                            st_sq(j - 1)
                        yield
                        if 0 <= j - 1 < nt:
                            st_ss(j - 1)
                        if j < nt:
                            st_mm(j)
                        if 0 <= j - 1 < nt:
                            st_le(j - 1)
                            st_out(j - 1)
                        yield
                    for bq in range(4):
                        b = 4 + PJ_BANKS.next()
                        for bl in range(4):
                            B = 4 * bq + bl
                            r, jb = B // nb, B % nb
                            for k in range(KD):
                                if d == 1:
                                    lh = xn[:, k, B * 128:(B + 1) * 128]
                                else:
                                    lh = xn[:, k, :].rearrange("p (l r) -> p r l", r=d)[:, r, jb * 128:(jb + 1) * 128]
                                MM(banks[b][:, bl * 128:(bl + 1) * 128], lh, wqkv[:, pg, 2, k, :], k == 0, k == KD - 1,
                                   [("wqkv", pg, 2)] + [("xn", k, t) for t in range(NT)], [BK(b)],
                                   sig=(k == KD - 1 and bl == 3))
                        yield
                        for h in range(2):
                            ACT(va[:, pg, 4 * bq:4 * bq + 4, h, h * 64:(h + 1) * 64],
                                banks[b][:, :].rearrange("p (b c) -> p b c", b=4)[:, :, h * 64:(h + 1) * 64], AF.Copy,
                                [("vaones", pg)], [BK(b), ("va", pg, bq)])
                        yield

                NBLK = 16 * len(units)
                cinfo = {}

                def c_s(i):
                    u, B = i // 16, i % 16
                    p, g = units[u]
                    pg = u % 2
                    nb = (S // DIL[g]) // 128
                    jb = B % nb
                    sbk = 2 + (i % 2)
                    cs = slice(B * 128, (B + 1) * 128)
                    for h in range(2):
                        if jb > 0:
                            MM(banks[sbk][:, h * 256:h * 256 + 128], kT[:, pg, (B - 1) * 128:B * 128], qz[:, pg, h, cs],
                               True, True, [("qk", 0, pg), ("qk", 1, pg)], [BK(sbk)], sig=False)
                        MM(banks[sbk][:, h * 256 + 128:h * 256 + 256], kT[:, pg, cs], qz[:, pg, h, cs], True, True,
                           [("qk", 0, pg), ("qk", 1, pg)], [BK(sbk)], sig=(h == 1))

                def c_add(i):
                    u, B = i // 16, i % 16
                    p, g = units[u]
                    nb = (S // DIL[g]) // 128
                    lo = 0 if (B % nb) > 0 else 128
                    sbk = 2 + (i % 2)
                    si = i % 3
                    TT(st[:, si, :, lo:256], banks[sbk][:, :].rearrange("p (h c) -> p h c", h=2)[:, :, lo:256],
                       btb[:, u % 3, :, lo:256], ALU.add, [("bt", u % 3)], [BK(sbk), ("st", si)])

                def c_exp(i):
                    u, B = i // 16, i % 16
                    p, g = units[u]
                    nb = (S // DIL[g]) // 128
                    lo = 0 if (B % nb) > 0 else 128
                    si = i % 3
                    ACT(pt[:, si, :, lo:256], st[:, si, :, lo:256], AF.Exp, [("st", si)], [("pt", si)])

                def c_pv(i):
                    u, B = i // 16, i % 16
                    p, g = units[u]
                    pg = u % 2
                    d = DIL[g]
                    nb = (S // d) // 128
                    r, jb = B // nb, B % nb
                    si = i % 3
                    for h in range(2):
                        obk = h
                        oreg = banks[obk][:, (B % 4) * 128:(B % 4 + 1) * 128]
                        if jb > 0:
                            MM(oreg, va[:, pg, B - 1, h, :], pt[:, si, h, 0:128], True, False,
                               [("pt", si), ("va", pg, (B - 1) // 4), ("vaones", pg)], [BK(obk)], sig=False)
                        MM(oreg, va[:, pg, B, h, :], pt[:, si, h, 128:256], jb == 0, True,
                           [("pt", si), ("va", pg, B // 4), ("vaones", pg)], [BK(obk)])
                    if B % 4 == 3:
                        B0 = B - 3
                        for h in range(2):
                            obk = h
                            if d == 1:
                                o_ap = oacc[:, h, B0 * 128:(B0 + 4) * 128]
                                i_ap = banks[obk][:, :]
                            elif d == 4:
                                o_ap = oacc[:, h, :].rearrange("p (l r) -> p r l", r=4)[:, r, :]
                                i_ap = banks[obk][:, :]
                            else:
                                o_ap = oacc[:, h, :].rearrange("p (l r) -> p r l", r=16)[:, B0:B0 + 4, :]
                                i_ap = banks[obk][:, :].rearrange("p (r l) -> p r l", r=4)
                            if g == 0:
                                ACT(o_ap, i_ap, AF.Copy, [], [BK(obk), ("oacc", h)])
                            else:
                                TT(o_ap, i_ap, o_ap, ALU.add, [], [BK(obk), ("oacc", h)])

                def fin_steps(p):
                    P.dma("pool", "awo", wo[:, :], awo_d[p * 128:(p + 1) * 128, :], writes=[("awo",)])
                    for t in range(NT):
                        ts = TSL[t]
                        ri = t % 2
                        ACT(oacc[64:128, 0, ts], oacc[64:128, 0, ts], AF.Ln, [("oacc", 0)], [("oacc", 0)])
                        ACT(rec[0:64, ri, :], oacc[64:128, 0, ts], AF.Exp, [("oacc", 0)], [("rec", ri, 0)], scale=-1.0)
                        ACT(oacc[0:64, 1, ts], oacc[0:64, 1, ts], AF.Ln, [("oacc", 1)], [("oacc", 1)])
                        ACT(rec[64:128, ri, :], oacc[0:64, 1, ts], AF.Exp, [("oacc", 1)], [("rec", ri, 1)], scale=-1.0)
                        TT(oT[0:64, ts], oacc[0:64, 0, ts], rec[0:64, ri, :], ALU.mult, [("oacc", 0), ("rec", ri, 0)], [("oT", t, 0)])
                        TT(oT[64:128, ts], oacc[64:128, 1, ts], rec[64:128, ri, :], ALU.mult, [("oacc", 1), ("rec", ri, 1)], [("oT", t, 1)])
                        yield
                    for dc in range(KD):
                        for t in range(NT):
                            b = 7
                            MM(banks[b][:, :], wo[:, dc * 128:(dc + 1) * 128], oT[:, TSL[t]], True, True,
                               [("awo",), ("oT", t, 0), ("oT", t, 1)], [BK(b)])
                            TT(xT[:, dc, TSL[t]], banks[b][:, :], xT[:, dc, TSL[t]], ALU.add, [], [BK(b), ("x", dc, t)])
                            yield

                load_unit(0)
                for _ in proj_steps(0):
                    pass
                projg = None
                fing = []

                def proj_next():
                    nonlocal projg
                    if projg is not None:
                        try:
                            next(projg)
                        except StopIteration:
                            projg = None

                for i in range(NBLK + 3):
                    if i < NBLK and i % 16 == 1 and i // 16 + 1 < len(units):
                        projg = proj_steps(i // 16 + 1)
                    proj_next()
                    if 0 <= i - 1 < NBLK:
                        c_add(i - 1)
                    if 0 <= i - 2 < NBLK:
                        c_exp(i - 2)
                    if 0 <= i - 3 < NBLK:
                        c_pv(i - 3)
                        u3, B3 = (i - 3) // 16, (i - 3) % 16
                        if B3 == 15 and units[u3][1] == 2:
                            fing.append((fin_steps(units[u3][0]), [0]))
                    if i < NBLK:
                        c_s(i)
                    proj_next()
                    if fing:
                        gen, cnt = fing[0]
                        for _ in range(1 if cnt[0] < NT else 3):
                            try:
                                next(gen)
                                cnt[0] += 1
                            except StopIteration:
                                fing.pop(0)
                                break
                for fg, _cnt in fing:
                    for _ in fg:
                        pass
                P.barrier()

        P.barrier()
        for ph in PHASES:
            if ph == "ffn0":
                ffn(0, PV_G + 0 * 8)
            elif ph == "rnn":
                rnn(PV_G + 1 * 8)
            elif ph == "ffn1":
                ffn(1, PV_G + 2 * 8)
            elif ph == "ffn2":
                ffn(2, PV_G + 3 * 8)
            elif ph == "att":
                attention(PV_G + 4 * 8)
            elif ph == "ffn3":
                ffn(3, PV_G + 5 * 8)

        out_dv = out_d.rearrange("(k p) t -> p k t", p=128)
        for k in range(KD):
            P.dma("sp", "xout", out_dv[:, k, :], xT[:, k, :], reads=[("x", k, t) for t in range(NT)])
        P.final_wait("sp", "xout")

        with nc.Block() as block:
            @block.tensor
            def _(e):
                P.emit("pe", e)

            @block.scalar
            def _(e):
                P.emit("act", e)

            @block.vector
            def _(e):
                P.emit("dve", e)

            @block.gpsimd
            def _(e):
                P.emit("pool", e)

            @block.sync
            def _(e):
                P.emit("sp", e)
    return nc


PHASES = ("ffn0", "rnn", "ffn1", "ffn2", "att", "ffn3")
DBG = {}


def _bucket_table():
    k = np.arange(128)[:, None]
    j = np.arange(256)[None, :]
    dist = np.where(j < 128, j + 128 - k, (j - 128) - k)
    valid = (dist >= 0) & (dist <= 128)
    tabs = []
    for d in DIL:
        n = np.maximum(dist * d, 0).astype(np.int32)
        nf = np.maximum(n, 1).astype(np.float32)
        large = 16 + (np.log(nf / np.float32(16)) / np.float32(math.log(2048 / 16)) * np.float32(16)).astype(np.int32)
        large = np.minimum(large, 31)
        bk = np.where(n < 16, n, large)
        tabs.append(np.where(valid, bk, 32))
    return np.stack(tabs, 0)


_CACHE = {}


def kernel(x, norm_g, ffn_w_in, ffn_w_out, rnn_w_in, rnn_conv_w, rnn_conv_b, rnn_w_a, rnn_b_a,
           rnn_w_x, rnn_b_x, rnn_lambda, rnn_w_out, att_w_qkv, att_q_gain, att_k_gain, att_w_o, rel_bias):
    f = lambda a: np.ascontiguousarray(np.asarray(a, dtype=np.float32))
    x = f(x)
    pv = np.zeros((128, PV_N), np.float32)
    pv[:, PV_G:PV_G + 48] = f(norm_g).reshape(6, KD, 128).transpose(2, 0, 1).reshape(128, 48)
    pv[:, PV_CW:PV_CW + 40] = f(rnn_conv_w)[0].reshape(4, NRC, 128).transpose(2, 1, 0).reshape(128, 40)
    pv[:, PV_CB:PV_CB + 10] = f(rnn_conv_b)[0].reshape(NRC, 128).T
    pv[:, PV_BA:PV_BA + 10] = f(rnn_b_a)[0].reshape(NRC, 128).T
    pv[:, PV_BX:PV_BX + 10] = f(rnn_b_x)[0].reshape(NRC, 128).T
    pv[:, PV_LAM:PV_LAM + 10] = f(rnn_lambda)[0].reshape(NRC, 128).T
    pv[:, PV_QG] = np.tile(f(att_q_gain)[0], 2)
    pv[:, PV_KG] = np.tile(f(att_k_gain)[0], 2)
    rb = np.concatenate([f(rel_bias), np.full((1, 48), NEG, np.float32)], 0)
    bk = _bucket_table()
    bt = np.empty((128, 48, 256), np.float32)
    for g in range(3):
        bt[:, g * 16:(g + 1) * 16, :] = rb[bk[g]][:, :, g * 16:(g + 1) * 16].transpose(0, 2, 1)
    shared = {
        "pv": pv, "bt": bt,
        "ffn_w_in": f(ffn_w_in).reshape(4 * D, 2 * DFF),
        "ffn_w_out": f(ffn_w_out).reshape(4 * DFF, D),
        "rnn_w_in": f(rnn_w_in).reshape(D, 2 * DRNN),
        "rnn_w_a": f(rnn_w_a).reshape(DRNN, 128),
        "rnn_w_x": f(rnn_w_x).reshape(DRNN, 128),
        "rnn_w_out": f(rnn_w_out).reshape(DRNN, D),
        "att_w_qkv": f(att_w_qkv).reshape(D, 9216),
        "att_w_o": f(att_w_o).reshape(D, D),
    }
    in_maps = []
    for c in range(N_CORES):
        m = dict(shared)
        m["xT"] = np.ascontiguousarray(x[c].T)
        in_maps.append(m)
    if "nc" not in _CACHE:
        _CACHE["nc"] = build_program()
    res = run_bass_kernel_spmd(_CACHE["nc"], in_maps, core_ids=list(range(N_CORES)))
    out = np.stack([np.ascontiguousarray(res.results[c]["outT"].T) for c in range(N_CORES)], 0)
    return out.astype(np.float32)
```

```python
import math
from contextlib import ExitStack

import numpy as np
import concourse.bass as bass
import concourse.mybir as mybir
from concourse.bass_utils import run_bass_kernel_spmd

F32 = mybir.dt.float32
BF16 = mybir.dt.bfloat16
AF = mybir.ActivationFunctionType
ALU = mybir.AluOpType

D = 1024
S = 2048
KD = D // 128
NT = S // 512
DFF = 2816
NFC = DFF // 128
DRNN = 1280
NRC = DRNN // 128
EPS = 1e-6
NEG = -30000.0
DIL = (1, 4, 16)
N_CORES = 8

PV_G = 0
PV_CW = 48
PV_CB = 88
PV_BA = 98
PV_BX = 108
PV_LAM = 118
PV_QG = 128
PV_KG = 129
PV_N = 130


class Prog:
    ENG = ("pe", "act", "dve", "pool", "sp")
    CENG = ("pe", "act", "dve", "pool")

    def __init__(self, nc, es):
        self.nc = nc
        self.es = es
        self.q = {e: [] for e in self.ENG}
        self.seq = {e: 0 for e in self.ENG}
        self.waited = {e: {} for e in self.ENG}
        self.lastw = {}
        self.readers = {}
        self.sems = {}
        self.dcount = {}
        self.needed = {e: set() for e in self.CENG}
        for e in self.CENG:
            self.sems[e] = es.enter_context(nc.semaphore("s_" + e))

    def _sem(self, sk):
        if sk not in self.sems:
            self.sems[sk] = self.es.enter_context(self.nc.semaphore("d_" + sk))
        return self.sems[sk]

    def _wait(self, eng, sk, v):
        if self.waited[eng].get(sk, 0) >= v:
            return
        self.waited[eng][sk] = v
        self.q[eng].append(("wait", sk, v))
        if sk in self.needed:
            self.needed[sk].add(v)

    def _deps(self, eng, reads, writes):
        deps = {}

        def add(d):
            if d is None:
                return
            sk, v = d
            if deps.get(sk, 0) < v:
                deps[sk] = v

        for k in reads:
            add(self.lastw.get(k))
        for k in writes:
            add(self.lastw.get(k))
            for sk, v in self.readers.get(k, {}).items():
                add((sk, v))
        for sk, v in deps.items():
            if sk == eng and eng == "pe":
                continue
            self._wait(eng, sk, v)

    def _mark(self, sk, val, reads, writes):
        for k in writes:
            self.lastw[k] = (sk, val)
            self.readers[k] = {}
        for k in reads:
            r = self.readers.setdefault(k, {})
            if r.get(sk, 0) < val:
                r[sk] = val

    def op(self, eng, name, kw, reads=(), writes=(), sig=True):
        self._deps(eng, reads, writes)
        if sig:
            self.seq[eng] += 1
            oid = self.seq[eng]
        else:
            oid = None
        self.q[eng].append(("op", (name, kw), oid, sig))
        self._mark(eng, self.seq[eng] if sig else self.seq[eng] + 1, reads, writes)

    def barrier(self):
        for eng in self.ENG:
            for f in self.CENG:
                if f == eng or self.seq[f] == 0:
                    continue
                self._wait(eng, f, self.seq[f])

    def dma(self, eng, stream, out, in_, reads=(), writes=()):
        self._sem(stream)
        self._deps(eng, reads, writes)
        self.dcount[stream] = self.dcount.get(stream, 0) + 16
        self.q[eng].append(("dma", out, in_, stream))
        self._mark(stream, self.dcount[stream], reads, writes)

    def final_wait(self, eng, stream):
        self.q[eng].append(("wait", stream, self.dcount[stream]))

    def _rank(self, eng):
        ids = sorted(self.needed[eng])
        return {v: i + 1 for i, v in enumerate(ids)}

    def emit(self, eng, e):
        my = self.sems.get(eng)
        ranks = {f: self._rank(f) for f in self.CENG}
        myneed = self.needed.get(eng, set())
        for item in self.q[eng]:
            if item[0] == "wait":
                sk, v = item[1], item[2]
                if sk in ranks:
                    v = ranks[sk][v]
                e.wait_ge(self._sem(sk), v)
            elif item[0] == "op":
                name, kw = item[1]
                ins = getattr(e, name)(**kw)
                if item[2] is not None and item[2] in myneed:
                    ins.then_inc(my, 1)
            else:
                e.dma_start(out=item[1], in_=item[2]).then_inc(self._sem(item[3]), 16)


class Ring:
    def __init__(self, n):
        self.n = n
        self.i = -1

    def next(self):
        self.i = (self.i + 1) % self.n
        return self.i


def build_program():
    nc = bass.Bass("TRN2", target_bir_lowering=False)

    def din(name, shape):
        return nc.dram_tensor(name, list(shape), F32, kind="ExternalInput").ap()

    xT_d = din("xT", [D, S])
    pv_d = din("pv", [128, PV_N])
    bt_d = din("bt", [128, 48, 256])
    fwi_d = din("ffn_w_in", [4 * D, 2 * DFF])
    fwo_d = din("ffn_w_out", [4 * DFF, D])
    rwi_d = din("rnn_w_in", [D, 2 * DRNN])
    rwa_d = din("rnn_w_a", [DRNN, 128])
    rwx_d = din("rnn_w_x", [DRNN, 128])
    rwo_d = din("rnn_w_out", [DRNN, D])
    aqkv_d = din("att_w_qkv", [D, 9216])
    awo_d = din("att_w_o", [D, D])
    out_d = nc.dram_tensor("outT", [D, S], F32, kind="ExternalOutput").ap()

    with ExitStack() as es:
        P = Prog(nc, es)

        def sb(name, shape, dt):
            return es.enter_context(nc.sbuf_tensor(name, list(shape), dt))

        xT = sb("xT_sb", [128, KD, S], F32)
        xn = sb("xn_sb", [128, KD, S], BF16)
        pv = sb("pv_sb", [128, PV_N], F32)
        dv = sb("dv_sb", [128, 64], F32)
        ones_bf = sb("ones_bf", [128, 128], BF16)
        bd_bf = sb("bd_bf", [128, 128], BF16)
        banks = [es.enter_context(nc.psum_tensor("bank%d" % i, [128, 512], F32)) for i in range(8)]
        DV_EPS, DV_ONE, DV_NLN8, DV_NEG1, DV_HBA, DV_HBX, DV_C4, DV_TMP = 0, 1, 2, 3, 4, 14, 24, 34

        bank_rr = Ring(8)

        def BK(i):
            return ("bank", i)

        def MM(out, lhsT, rhs, start, stop, reads, writes, sig=True):
            P.op("pe", "matmul", dict(out=out, lhsT=lhsT, rhs=rhs, start=start, stop=stop), reads, writes, sig)

        def ACT(out, in_, func, reads, writes, bias=None, scale=None):
            kw = dict(out=out, in_=in_, func=func)
            if bias is not None:
                kw["bias"] = bias
            if scale is not None:
                kw["scale"] = scale
            P.op("act", "activation", kw, reads, writes)

        def TS(out, in0, s1, s2, op0, op1, reads, writes):
            kw = dict(out=out, in0=in0, scalar1=s1, scalar2=s2, op0=op0)
            if op1 is not None:
                kw["op1"] = op1
            P.op("dve", "tensor_scalar", kw, reads, writes)

        def STT(out, in0, scalar, in1, op0, op1, reads, writes):
            P.op("dve", "scalar_tensor_tensor", dict(out=out, in0=in0, scalar=scalar, in1=in1, op0=op0, op1=op1),
                 reads, writes)

        def TT(out, in0, in1, op, reads, writes):
            P.op("dve", "tensor_tensor", dict(out=out, in0=in0, in1=in1, op=op), reads, writes)

        def RECIP(out, in_, reads, writes):
            P.op("dve", "reciprocal", dict(out=out, in_=in_), reads, writes)

        def RECIP_FAST(out, in_, reads, writes):
            P.op("dve", "reciprocal_approx_fast", dict(out=out, in_=in_), reads, writes)

        def MEMSET(ap, val, writes):
            P.op("dve", "memset", dict(ap=ap, constant=val), (), writes)

        def col(t, c, n=1):
            return t[:, c:c + n]

        xT_dv = xT_d.rearrange("(k p) t -> p k t", p=128)
        P.dma("sp", "pvin", pv[:], pv_d, writes=["pv"])
        for t in range(NT):
            P.dma("sp", "xin%d" % t, xT[:, :, t * 512:(t + 1) * 512], xT_dv[:, :, t * 512:(t + 1) * 512],
                  writes=[("x", k, t) for k in range(KD)])

        MEMSET(ones_bf[:], 1.0, ["ones"])
        MEMSET(bd_bf[:], 0.0, ["bd"])
        MEMSET(bd_bf[0:64, 0:64], 1.0, ["bd"])
        MEMSET(bd_bf[64:128, 64:128], 1.0, ["bd"])
        MEMSET(col(dv, DV_EPS), EPS, ["dvc"])
        MEMSET(col(dv, DV_ONE), 1.0, ["dvc"])
        MEMSET(col(dv, DV_NLN8), -math.log(8.0), ["dvc"])
        MEMSET(col(dv, DV_NEG1), -1.0, ["dvc"])
        TS(col(dv, DV_HBA, 20), col(pv, PV_BA, 20), 0.5, None, ALU.mult, None, ["pv"], ["dvh"])
        ACT(col(dv, DV_TMP, 10), col(pv, PV_LAM, 10), AF.Exp, ["pv"], ["dvt"], scale=-1.0)
        ACT(col(dv, DV_TMP + 10, 10), col(dv, DV_TMP, 10), AF.Ln, ["dvt", "dvc"], ["dvt2"], bias=col(dv, DV_ONE), scale=1.0)
        TS(col(dv, DV_C4, 10), col(dv, DV_TMP + 10, 10), 2.0, None, ALU.mult, None, ["dvt2"], ["dvc4"])

        TSL = [slice(t * 512, (t + 1) * 512) for t in range(NT)]

        def rmsnorm(gcol, tmp):
            sq, lnt, rs = tmp
            sq_rr, ln_rr = Ring(4), Ring(2)
            for t in range(NT):
                ts = TSL[t]
                b = bank_rr.next()
                for k in range(KD):
                    s = sq_rr.next()
                    ACT(sq[:, s, :], xT[:, k, ts], AF.Square, [("x", k, t)], [("sq", s)])
                    MM(banks[b][:, :], ones_bf[:, :], sq[:, s, :], k == 0, k == KD - 1,
                       [("sq", s), "ones"], [BK(b)])
                l = ln_rr.next()
                ACT(lnt[:, l, :], banks[b][:, :], AF.Ln, ["dvc"], [BK(b), ("lnt", l)], bias=col(dv, DV_EPS), scale=1.0 / D)
                ACT(rs[:, l, :], lnt[:, l, :], AF.Exp, [("lnt", l)], [("rs", l)], scale=-0.5)
                for k in range(KD):
                    STT(xn[:, k, ts], xT[:, k, ts], col(pv, gcol + k), rs[:, l, :], ALU.mult, ALU.mult,
                        [("x", k, t), ("rs", l), "pv"], [("xn", k, t)])

        def ffn(idx, gcol):
            with ExitStack() as fs:
                def fsb(name, shape, dt):
                    return fs.enter_context(nc.sbuf_tensor("%s_%d" % (name, idx), list(shape), dt))
                NSL = 2
                CPS = NFC // NSL
                gT = fsb("gT", [128, CPS, S], BF16)
                win = fsb("win", [128, 4, 2, KD, 128], BF16)
                wout = fsb("wout", [128, 4, CPS, 128], BF16)
                sq = fsb("sq", [128, 4, 512], BF16)
                lnt = fsb("lnt", [128, 2, 512], F32)
                rs = fsb("rs", [128, 2, 512], F32)
                sg = fsb("sg", [128, 3, 512], F32)
                rmsnorm(gcol, (sq, lnt, rs))
                if DBG.get("ffn_stop") == "norm":
                    P.barrier()
                    return
                wi_v = fwi_d[idx * D:(idx + 1) * D, :].rearrange("(k p) f -> p k f", p=128)
                wo_v = fwo_d[idx * DFF:(idx + 1) * DFF, :].rearrange("(c p) d -> p c d", p=128)
                win_rr, wout_rr, sg_rr = Ring(4), Ring(4), Ring(3)
                def load_win(c):
                    ws = win_rr.next()
                    for gu in range(2):
                        c0 = gu * DFF + c * 128
                        P.dma("pool", "win%d_%d" % (ws, gu), win[:, ws, gu, :, :], wi_v[:, :, c0:c0 + 128],
                              writes=[("win", ws, gu)])
                    return ws

                def up_unit(ci, t, ws):
                    ts = TSL[t]
                    bg = bank_rr.next()
                    bu = bank_rr.next()
                    for gu, b in ((0, bg), (1, bu)):
                        for k in range(KD):
                            MM(banks[b][:, :], win[:, ws, gu, k, :], xn[:, k, ts], k == 0, k == KD - 1,
                               [("win", ws, gu), ("xn", k, t)], [BK(b)], sig=(k == KD - 1))
                    s = sg_rr.next()
                    ACT(sg[:, s, :], banks[bg][:, :], AF.Silu, [], [BK(bg), ("sg", s)])
                    TT(gT[:, ci, ts], sg[:, s, :], banks[bu][:, :], ALU.mult, [("sg", s)], [BK(bu), ("gT", ci, t)])

                for sl in range(NSL):
                    ci0 = 0
                    if sl == 0:
                        wsl0 = [load_win(sl * CPS + ci) for ci in range(3)]
                        for t in range(NT):
                            for ci in range(3):
                                up_unit(ci, t, wsl0[ci])
                        ci0 = 3
                    for ci in range(ci0, CPS):
                        ws = load_win(sl * CPS + ci)
                        for t in range(NT):
                            up_unit(ci, t, ws)
                    if DBG.get("ffn_stop") == "up":
                        continue
                    for dc in range(KD):
                        ws = wout_rr.next()
                        for hf, (c_lo, c_hi) in enumerate(((0, 6), (6, CPS))):
                            P.dma("pool", "wout%d_%d" % (ws, hf), wout[:, ws, c_lo:c_hi, :],
                                  wo_v[:, sl * CPS + c_lo:sl * CPS + c_hi, dc * 128:(dc + 1) * 128],
                                  writes=[("wout", ws, hf)])
                        for t in range(NT):
                            ts = TSL[t]
                            b = bank_rr.next()
                            for ci in range(CPS):
                                MM(banks[b][:, :], wout[:, ws, ci, :], gT[:, ci, ts], ci == 0, ci == CPS - 1,
                                   [("wout", ws, 0 if ci < 6 else 1), ("gT", ci, t)], [BK(b)], sig=(ci == CPS - 1))
                            STT(xT[:, dc, ts], banks[b][:, :], 0.5, xT[:, dc, ts], ALU.mult, ALU.add,
                                [], [BK(b), ("x", dc, t)])
                P.barrier()

        def rnn(gcol):
            with ExitStack() as fs:
                def fsb(name, shape, dt):
                    return fs.enter_context(nc.sbuf_tensor("r_" + name, list(shape), dt))
                sq = fsb("sq", [128, 4, 512], BF16)
                lnt = fsb("lnt", [128, 2, 512], F32)
                rs = fsb("rs", [128, 2, 512], F32)
                wgu = fsb("wgu", [128, 3, 2, KD, 128], BF16)
                wax = fsb("wax", [128, NRC, 2, 128], BF16)
                wo = fsb("wo", [128, NRC, D], BF16)
                gg = fsb("gg", [128, 3, 512], F32)
                ub = fsb("ub", [128, 2, 516], F32)
                cv = fsb("cv", [128, 2, 512], F32)
                cvb = fsb("cvb", [128, 2, 512], BF16)
                rp = fsb("rp", [128, 2, 512], F32)
                ip = fsb("ip", [128, 2, 512], F32)
                uu = fsb("uu", [128, 2, 512], F32)
                ww = fsb("ww", [128, 2, 512], F32)
                yb = fsb("yb", [128, 2, NRC, 512], BF16)
                ucar = fsb("ucar", [128, NRC, 4], F32)
                hcar = fsb("hcar", [128, NRC], F32)
                rmsnorm(gcol, (sq, lnt, rs))
                MEMSET(ucar[:, :, :], 0.0, [("ucar", n) for n in range(NRC)])
                wa_v = rwa_d.rearrange("(n p) d -> p n d", p=128)
                wx_v = rwx_d.rearrange("(n p) d -> p n d", p=128)
                wo_v = rwo_d.rearrange("(n p) d -> p n d", p=128)
                for hf in range(2):
                    ns = slice(hf * 5, hf * 5 + 5)
                    P.dma("pool", "rwa%d" % hf, wax[:, ns, 0, :], wa_v[:, ns, :], writes=[("wax", 0, hf)])
                    P.dma("pool", "rwx%d" % hf, wax[:, ns, 1, :], wx_v[:, ns, :], writes=[("wax", 1, hf)])
                    P.dma("pool", "rwo%d" % hf, wo[:, ns, :], wo_v[:, ns, :], writes=[("wo", hf)])
                wi_v = rwi_d.rearrange("(k p) f -> p k f", p=128)
                NU = NT * NRC
                units = [(t, n) for t in range(NT) for n in range(NRC)]

                def load_w(k):
                    t, n = units[k]
                    ws = k % 3
                    for gu in range(2):
                        c0 = gu * DRNN + n * 128
                        P.dma("pool", "rwgu%d_%d" % (ws, gu), wgu[:, ws, gu, :, :], wi_v[:, :, c0:c0 + 128],
                              writes=[("wgu", ws, gu)])

                def stage_a(k):
                    t, n = units[k]
                    pb, ws, ts = k % 2, k % 3, TSL[t]
                    b = bank_rr.next()
                    for kd in range(KD):
                        MM(banks[b][:, :], wgu[:, ws, 0, kd, :], xn[:, kd, ts], kd == 0, kd == KD - 1,
                           [("wgu", ws, 0), ("xn", kd, t)], [BK(b)], sig=(kd == KD - 1))
                    ACT(gg[:, k % 3, :], banks[b][:, :], AF.Gelu_apprx_tanh, [], [BK(b), ("gg", k % 3)])
                    b = bank_rr.next()
                    for kd in range(KD):
                        MM(banks[b][:, :], wgu[:, ws, 1, kd, :], xn[:, kd, ts], kd == 0, kd == KD - 1,
                           [("wgu", ws, 1), ("xn", kd, t)], [BK(b)], sig=(kd == KD - 1))
                    ACT(ub[:, pb, 3:515], banks[b][:, :], AF.Copy, [], [BK(b), ("ub", pb)])
                    ACT(ub[:, pb, 0:3], ucar[:, n, 0:3], AF.Copy, [("ucar", n)], [("ub", pb)])
                    ACT(ucar[:, n, 0:3], ub[:, pb, 512:515], AF.Copy, [("ub", pb)], [("ucar", n)])

                def stage_c1(k):
                    t, n = units[k]
                    pb = k % 2
                    cw = PV_CW + n * 4
                    TS(cv[:, pb, :], ub[:, pb, 0:512], col(pv, cw), col(pv, PV_CB + n), ALU.mult, ALU.add,
                       [("ub", pb), "pv"], [("cv", pb)])
                    for kk in range(1, 4):
                        STT(cv[:, pb, :], ub[:, pb, kk:kk + 512], col(pv, cw + kk), cv[:, pb, :], ALU.mult, ALU.add,
                            [("ub", pb), "pv"], [("cv", pb)])

                def stage_c2(k):
                    t, n = units[k]
                    pb = k % 2
                    ACT(cvb[:, pb, :], cv[:, pb, :], AF.Copy, [("cv", pb)], [("cvb", pb)])
                    br = bank_rr.next()
                    MM(banks[br][:, :], wax[:, n, 0, :], cvb[:, pb, :], True, True, [("wax", 0, n // 5), ("cvb", pb)], [BK(br)])
                    bi = bank_rr.next()
                    MM(banks[bi][:, :], wax[:, n, 1, :], cvb[:, pb, :], True, True, [("wax", 1, n // 5), ("cvb", pb)], [BK(bi)])
                    ACT(rp[:, pb, :], banks[br][:, :], AF.Tanh, ["dvh"], [BK(br), ("rp", pb)], bias=col(dv, DV_HBA + n), scale=0.5)
                    ACT(ip[:, pb, :], banks[bi][:, :], AF.Tanh, ["dvh"], [BK(bi), ("ip", pb)], bias=col(dv, DV_HBX + n), scale=0.5)
                    ACT(uu[:, pb, :], rp[:, pb, :], AF.Tanh, [("rp", pb), "dvc4"], [("uu", pb)],
                        bias=col(dv, DV_C4 + n), scale=col(dv, DV_C4 + n))
                    ACT(ww[:, pb, :], uu[:, pb, :], AF.Identity, [("uu", pb), "dvc"], [("ww", pb)], bias=col(dv, DV_ONE), scale=1.0)

                def stage_e1(k):
                    pb = k % 2
                    RECIP(ww[:, pb, :], ww[:, pb, :], [("ww", pb)], [("ww", pb)])

                def stage_e2(k):
                    t, n = units[k]
                    pb = k % 2
                    ACT(rp[:, pb, :], ww[:, pb, :], AF.Identity, [("ww", pb), "dvc"], [("rp", pb)], bias=col(dv, DV_NEG1), scale=2.0)
                    ACT(uu[:, pb, :], uu[:, pb, :], AF.Sqrt, [("uu", pb)], [("uu", pb)])
                    STT(ip[:, pb, :], ip[:, pb, :], 1.0, cv[:, pb, :], ALU.add, ALU.mult, [("ip", pb), ("cv", pb)], [("ip", pb)])
                    P.op("pool", "tensor_tensor", dict(out=uu[:, pb, :], in0=uu[:, pb, :], in1=ww[:, pb, :], op=ALU.mult),
                         [("uu", pb), ("ww", pb)], [("uu", pb)])

                def stage_e3(k):
                    t, n = units[k]
                    pb = k % 2
                    TT(ip[:, pb, :], ip[:, pb, :], uu[:, pb, :], ALU.mult, [("uu", pb), ("ip", pb)], [("ip", pb)])
                    init = 0.0 if t == 0 else hcar[:, n:n + 1]
                    P.op("dve", "tensor_tensor_scan",
                         dict(out=cv[:, pb, :], data0=rp[:, pb, :], data1=ip[:, pb, :], initial=init, op0=ALU.mult, op1=ALU.add),
                         [("rp", pb), ("ip", pb), ("hcar", n)], [("cv", pb)])
                    if t + 1 < NT:
                        ACT(hcar[:, n:n + 1], cv[:, pb, 511:512], AF.Copy, [("cv", pb)], [("hcar", n)])
                    P.op("pool", "tensor_tensor", dict(out=yb[:, t % 2, n, :], in0=cv[:, pb, :], in1=gg[:, k % 3, :], op=ALU.mult),
                         [("cv", pb), ("gg", k % 3)], [("yb", t % 2, n)])
                    if n == NRC - 1:
                        for dc in range(KD):
                            b = bank_rr.next()
                            for m in range(NRC):
                                MM(banks[b][:, :], wo[:, m, dc * 128:(dc + 1) * 128], yb[:, t % 2, m, :], m == 0, m == NRC - 1,
                                   [("wo", m // 5), ("yb", t % 2, m)], [BK(b)], sig=(m == NRC - 1))
                            TT(xT[:, dc, TSL[t]], banks[b][:, :], xT[:, dc, TSL[t]], ALU.add, [], [BK(b), ("x", dc, t)])

                for k in range(min(3, NU)):
                    load_w(k)
                stage_a(0)
                for k in range(NU + 1):
                    if k + 3 < NU:
                        load_w(k + 3)
                    if k + 1 < NU:
                        stage_a(k + 1)
                    if k - 1 >= 0:
                        stage_e1(k - 1)
                    if k < NU:
                        stage_c1(k)
                    if k - 1 >= 0:
                        stage_e2(k - 1)
                    if k < NU:
                        stage_c2(k)
                    if k - 1 >= 0:
                        stage_e3(k - 1)
                P.barrier()

        def attention(gcol):
            with ExitStack() as fs:
                def fsb(name, shape, dt):
                    return fs.enter_context(nc.sbuf_tensor("a_" + name, list(shape), dt))
                sq = fsb("sq", [128, 4, 512], BF16)
                lnt = fsb("lnt", [128, 2, 512], F32)
                rs = fsb("rs", [128, 2, 512], F32)
                wqkv = fsb("wqkv", [128, 2, 3, KD, 128], BF16)
                wo = fsb("wo", [128, D], BF16)
                btb = fsb("btb", [128, 3, 2, 256], F32)
                qz = fsb("qz", [128, 2, 2, S], BF16)
                qs = fsb("qs", [128, S], BF16)
                kT = fsb("kT", [128, 2, S], BF16)
                va = fsb("va", [128, 2, 16, 2, 128], BF16)
                oacc = fsb("oacc", [128, 2, S], F32)
                rec = fsb("rec", [128, 2, 512], F32)
                oT = fsb("oT", [128, S], BF16)
                st = fsb("st", [128, 3, 2, 256], F32)
                pt = fsb("pt", [128, 3, 2, 256], BF16)
                rmsnorm(gcol, (sq, lnt, rs))
                MEMSET(qz[:, :, :, :], 0.0, [("qk", 0, 0), ("qk", 0, 1)])
                for vb in range(2):
                    MEMSET(va[:, vb, :, 0, 64:128], 1.0, [("vaones", vb)])
                    MEMSET(va[:, vb, :, 1, 0:64], 1.0, [("vaones", vb)])
                wq_v = aqkv_d.rearrange("(k p) f -> p k f", p=128)
                sq_rr, ln_rr, st_rr = Ring(4), Ring(2), Ring(3)
                S_BANKS = Ring(2)
                PJ_BANKS = Ring(2)
                FIN_BANKS = Ring(2)
                SS_BANK, V_BANK = 6, 7
                units = [(p, g) for p in range(8) for g in range(3)]

                def load_unit(u):
                    p, g = units[u]
                    pg = u % 2
                    for qi in range(3):
                        c0 = (qi * 3 + g) * 1024 + p * 128
                        P.dma("pool", "wqkv%d_%d" % (pg, qi), wqkv[:, pg, qi, :, :], wq_v[:, :, c0:c0 + 128],
                              writes=[("wqkv", pg, qi)])
                    P.dma("sp", "bt%d" % (u % 3), btb[:, u % 3, :, :], bt_d[:, g * 16 + 2 * p:g * 16 + 2 * p + 2, :],
                          writes=[("bt", u % 3)])

                def proj_steps(u):
                    p, g = units[u]
                    pg = u % 2
                    d = DIL[g]
                    nb = (S // d) // 128
                    if u + 1 < len(units):
                        load_unit(u + 1)
                    tiles = [(qi, t) for qi in range(2) for t in range(NT)]
                    info = {}

                    def st_mm(j):
                        qi, t = tiles[j]
                        b = 4 + PJ_BANKS.next()
                        for k in range(KD):
                            MM(banks[b][:, :], wqkv[:, pg, qi, k, :], xn[:, k, TSL[t]], k == 0, k == KD - 1,
                               [("wqkv", pg, qi), ("xn", k, t)], [BK(b)], sig=(k == KD - 1))
                        info[j] = [b, None, None]

                    def st_sq(j):
                        b = info[j][0]
                        s = sq_rr.next()
                        ACT(sq[:, s, :], banks[b][:, :], AF.Square, [], [BK(b), ("sq", s)])
                        info[j][1] = s

                    def st_ss(j):
                        s = info[j][1]
                        MM(banks[SS_BANK][:, :], bd_bf[:, :], sq[:, s, :], True, True, [("sq", s), "bd"], [BK(SS_BANK)])

                    def st_le(j):
                        qi, t = tiles[j]
                        l = ln_rr.next()
                        ACT(lnt[:, l, :], banks[SS_BANK][:, :], AF.Ln, ["dvc"], [BK(SS_BANK), ("lnt", l)],
                            bias=col(dv, DV_EPS), scale=1.0 / 64)
                        ACT(rs[:, l, :], lnt[:, l, :], AF.Exp, [("lnt", l), "dvc"], [("rs", l)],
                            bias=(col(dv, DV_NLN8) if qi == 0 else None), scale=-0.5)
                        info[j][2] = l

                    def st_out(j):
                        qi, t = tiles[j]
                        b, s, l = info[j]
                        gain = col(pv, PV_QG) if qi == 0 else col(pv, PV_KG)
                        lc = 512 // d
                        if qi == 0:
                            parts = [(slice(0, 64), qz[0:64, pg, 0, :]), (slice(64, 128), qz[64:128, pg, 1, :])]
                        else:
                            parts = [(slice(0, 128), kT[:, pg, :])]
                        def region(full):
                            if d == 1:
                                return full[:, TSL[t]]
                            return full.rearrange("p (r l) -> p r l", r=d)[:, :, t * lc:(t + 1) * lc]

                        if qi == 0:
                            if d == 1:
                                i_ap, r_ap = banks[b][:, :], rs[:, l, :]
                            else:
                                i_ap = banks[b][:, :].rearrange("p (l r) -> p r l", r=d)
                                r_ap = rs[:, l, :].rearrange("p (l r) -> p r l", r=d)
                            qkeys = [("qs", t % 2)] + ([("qs", 1)] if t == 0 else [])
                            STT(region(qs[:, :]), i_ap, gain, r_ap, ALU.mult, ALU.mult, [("rs", l), "pv"], [BK(b)] + qkeys)
                            P.op("pool", "tensor_copy", dict(out=region(qz[0:64, pg, 0, :]), in_=region(qs[0:64, :])),
                                 [("qs", t % 2)], [("qk", 0, pg)])
                            P.op("pool", "tensor_copy", dict(out=region(qz[64:128, pg, 1, :]), in_=region(qs[64:128, :])),
                                 [("qs", t % 2)], [("qk", 0, pg)])
                            return
                        for ps_, dfull in parts:
                            if d == 1:
                                o_ap = dfull[:, TSL[t]]
                                i_ap = banks[b][ps_, :]
                                r_ap = rs[ps_, l, :]
                            else:
                                o_ap = dfull.rearrange("p (r l) -> p r l", r=d)[:, :, t * lc:(t + 1) * lc]
                                i_ap = banks[b][ps_, :].rearrange("p (l r) -> p r l", r=d)
                                r_ap = rs[ps_, l, :].rearrange("p (l r) -> p r l", r=d)
                            STT(o_ap, i_ap, gain[ps_, :], r_ap, ALU.mult, ALU.mult, [("rs", l), "pv"], [BK(b), ("qk", qi, pg)])

                    nt = len(tiles)
                    for j in range(nt + 1):
                        if 0 <= j - 1 < nt:
                            st_sq(j - 1)
                        yield
                        if 0 <= j - 1 < nt:
                            st_ss(j - 1)
                        if j < nt:
                            st_mm(j)
                        if 0 <= j - 1 < nt:
                            st_le(j - 1)
                            st_out(j - 1)
                        yield
                    for bq in range(4):
                        b = 4 + PJ_BANKS.next()
                        for bl in range(4):
                            B = 4 * bq + bl
                            r, jb = B // nb, B % nb
                            for k in range(KD):
                                if d == 1:
                                    lh = xn[:, k, B * 128:(B + 1) * 128]
                                else:
                                    lh = xn[:, k, :].rearrange("p (l r) -> p r l", r=d)[:, r, jb * 128:(jb + 1) * 128]
                                MM(banks[b][:, bl * 128:(bl + 1) * 128], lh, wqkv[:, pg, 2, k, :], k == 0, k == KD - 1,
                                   [("wqkv", pg, 2)] + [("xn", k, t) for t in range(NT)], [BK(b)],
                                   sig=(k == KD - 1 and bl == 3))
                        yield
                        for h in range(2):
                            ACT(va[:, pg, 4 * bq:4 * bq + 4, h, h * 64:(h + 1) * 64],
                                banks[b][:, :].rearrange("p (b c) -> p b c", b=4)[:, :, h * 64:(h + 1) * 64], AF.Copy,
                                [("vaones", pg)], [BK(b), ("va", pg, bq)])
                        yield

                NBLK = 16 * len(units)
                cinfo = {}

                def c_s(i):
                    u, B = i // 16, i % 16
                    p, g = units[u]
                    pg = u % 2
                    nb = (S // DIL[g]) // 128
                    jb = B % nb
                    sbk = 2 + (i % 2)
                    cs = slice(B * 128, (B + 1) * 128)
                    for h in range(2):
                        if jb > 0:
                            MM(banks[sbk][:, h * 256:h * 256 + 128], kT[:, pg, (B - 1) * 128:B * 128], qz[:, pg, h, cs],
                               True, True, [("qk", 0, pg), ("qk", 1, pg)], [BK(sbk)], sig=False)
                        MM(banks[sbk][:, h * 256 + 128:h * 256 + 256], kT[:, pg, cs], qz[:, pg, h, cs], True, True,
                           [("qk", 0, pg), ("qk", 1, pg)], [BK(sbk)], sig=(h == 1))

                def c_add(i):
                    u, B = i // 16, i % 16
                    p, g = units[u]
                    nb = (S // DIL[g]) // 128
                    lo = 0 if (B % nb) > 0 else 128
                    sbk = 2 + (i % 2)
                    si = i % 3
                    TT(st[:, si, :, lo:256], banks[sbk][:, :].rearrange("p (h c) -> p h c", h=2)[:, :, lo:256],
                       btb[:, u % 3, :, lo:256], ALU.add, [("bt", u % 3)], [BK(sbk), ("st", si)])

                def c_exp(i):
                    u, B = i // 16, i % 16
                    p, g = units[u]
                    nb = (S // DIL[g]) // 128
                    lo = 0 if (B % nb) > 0 else 128
                    si = i % 3
                    ACT(pt[:, si, :, lo:256], st[:, si, :, lo:256], AF.Exp, [("st", si)], [("pt", si)])

                def c_pv(i):
                    u, B = i // 16, i % 16
                    p, g = units[u]
                    pg = u % 2
                    d = DIL[g]
                    nb = (S // d) // 128
                    r, jb = B // nb, B % nb
                    si = i % 3
                    for h in range(2):
                        obk = h
                        oreg = banks[obk][:, (B % 4) * 128:(B % 4 + 1) * 128]
                        if jb > 0:
                            MM(oreg, va[:, pg, B - 1, h, :], pt[:, si, h, 0:128], True, False,
                               [("pt", si), ("va", pg, (B - 1) // 4), ("vaones", pg)], [BK(obk)], sig=False)
                        MM(oreg, va[:, pg, B, h, :], pt[:, si, h, 128:256], jb == 0, True,
                           [("pt", si), ("va", pg, B // 4), ("vaones", pg)], [BK(obk)])
                    if B % 4 == 3:
                        B0 = B - 3
                        for h in range(2):
                            obk = h
                            if d == 1:
                                o_ap = oacc[:, h, B0 * 128:(B0 + 4) * 128]
                                i_ap = banks[obk][:, :]
                            elif d == 4:
                                o_ap = oacc[:, h, :].rearrange("p (l r) -> p r l", r=4)[:, r, :]
                                i_ap = banks[obk][:, :]
                            else:
                                o_ap = oacc[:, h, :].rearrange("p (l r) -> p r l", r=16)[:, B0:B0 + 4, :]
                                i_ap = banks[obk][:, :].rearrange("p (r l) -> p r l", r=4)
                            if g == 0:
                                ACT(o_ap, i_ap, AF.Copy, [], [BK(obk), ("oacc", h)])
                            else:
                                TT(o_ap, i_ap, o_ap, ALU.add, [], [BK(obk), ("oacc", h)])

                def fin_steps(p):
                    P.dma("pool", "awo", wo[:, :], awo_d[p * 128:(p + 1) * 128, :], writes=[("awo",)])
                    for t in range(NT):
                        ts = TSL[t]
                        ri = t % 2
                        ACT(oacc[64:128, 0, ts], oacc[64:128, 0, ts], AF.Ln, [("oacc", 0)], [("oacc", 0)])
                        ACT(rec[0:64, ri, :], oacc[64:128, 0, ts], AF.Exp, [("oacc", 0)], [("rec", ri, 0)], scale=-1.0)
                        ACT(oacc[0:64, 1, ts], oacc[0:64, 1, ts], AF.Ln, [("oacc", 1)], [("oacc", 1)])
                        ACT(rec[64:128, ri, :], oacc[0:64, 1, ts], AF.Exp, [("oacc", 1)], [("rec", ri, 1)], scale=-1.0)
                        TT(oT[0:64, ts], oacc[0:64, 0, ts], rec[0:64, ri, :], ALU.mult, [("oacc", 0), ("rec", ri, 0)], [("oT", t, 0)])
                        TT(oT[64:128, ts], oacc[64:128, 1, ts], rec[64:128, ri, :], ALU.mult, [("oacc", 1), ("rec", ri, 1)], [("oT", t, 1)])
                        yield
                    for dc in range(KD):
                        for t in range(NT):
                            b = 7
                            MM(banks[b][:, :], wo[:, dc * 128:(dc + 1) * 128], oT[:, TSL[t]], True, True,
                               [("awo",), ("oT", t, 0), ("oT", t, 1)], [BK(b)])
                            TT(xT[:, dc, TSL[t]], banks[b][:, :], xT[:, dc, TSL[t]], ALU.add, [], [BK(b), ("x", dc, t)])
                            yield

                load_unit(0)
                for _ in proj_steps(0):
                    pass
                projg = None
                fing = []

                def proj_next():
                    nonlocal projg
                    if projg is not None:
                        try:
                            next(projg)
                        except StopIteration:
                            projg = None

                for i in range(NBLK + 3):
                    if i < NBLK and i % 16 == 1 and i // 16 + 1 < len(units):
                        projg = proj_steps(i // 16 + 1)
                    proj_next()
                    if 0 <= i - 1 < NBLK:
                        c_add(i - 1)
                    if 0 <= i - 2 < NBLK:
                        c_exp(i - 2)
                    if 0 <= i - 3 < NBLK:
                        c_pv(i - 3)
                        u3, B3 = (i - 3) // 16, (i - 3) % 16
                        if B3 == 15 and units[u3][1] == 2:
                            fing.append((fin_steps(units[u3][0]), [0]))
                    if i < NBLK:
                        c_s(i)
                    proj_next()
                    if fing:
                        gen, cnt = fing[0]
                        for _ in range(1 if cnt[0] < NT else 3):
                            try:
                                next(gen)
                                cnt[0] += 1
                            except StopIteration:
                                fing.pop(0)
                                break
                for fg, _cnt in fing:
                    for _ in fg:
                        pass
                P.barrier()

        P.barrier()
        for ph in PHASES:
            if ph == "ffn0":
                ffn(0, PV_G + 0 * 8)
            elif ph == "rnn":
                rnn(PV_G + 1 * 8)
            elif ph == "ffn1":
                ffn(1, PV_G + 2 * 8)
            elif ph == "ffn2":
                ffn(2, PV_G + 3 * 8)
            elif ph == "att":
                attention(PV_G + 4 * 8)
            elif ph == "ffn3":
                ffn(3, PV_G + 5 * 8)

        out_dv = out_d.rearrange("(k p) t -> p k t", p=128)
        for k in range(KD):
            P.dma("sp", "xout", out_dv[:, k, :], xT[:, k, :], reads=[("x", k, t) for t in range(NT)])
        P.final_wait("sp", "xout")

        with nc.Block() as block:
            @block.tensor
            def _(e):
                P.emit("pe", e)

            @block.scalar
            def _(e):
                P.emit("act", e)

            @block.vector
            def _(e):
                P.emit("dve", e)

            @block.gpsimd
            def _(e):
                P.emit("pool", e)

            @block.sync
            def _(e):
                P.emit("sp", e)
    return nc


PHASES = ("ffn0", "rnn", "ffn1", "ffn2", "att", "ffn3")
DBG = {}


def _bucket_table():
    k = np.arange(128)[:, None]
    j = np.arange(256)[None, :]
    dist = np.where(j < 128, j + 128 - k, (j - 128) - k)
    valid = (dist >= 0) & (dist <= 128)
    tabs = []
    for d in DIL:
        n = np.maximum(dist * d, 0).astype(np.int32)
        nf = np.maximum(n, 1).astype(np.float32)
        large = 16 + (np.log(nf / np.float32(16)) / np.float32(math.log(2048 / 16)) * np.float32(16)).astype(np.int32)
        large = np.minimum(large, 31)
        bk = np.where(n < 16, n, large)
        tabs.append(np.where(valid, bk, 32))
    return np.stack(tabs, 0)


_CACHE = {}


def kernel(x, norm_g, ffn_w_in, ffn_w_out, rnn_w_in, rnn_conv_w, rnn_conv_b, rnn_w_a, rnn_b_a,
           rnn_w_x, rnn_b_x, rnn_lambda, rnn_w_out, att_w_qkv, att_q_gain, att_k_gain, att_w_o, rel_bias):
    f = lambda a: np.ascontiguousarray(np.asarray(a, dtype=np.float32))
    x = f(x)
    pv = np.zeros((128, PV_N), np.float32)
    pv[:, PV_G:PV_G + 48] = f(norm_g).reshape(6, KD, 128).transpose(2, 0, 1).reshape(128, 48)
    pv[:, PV_CW:PV_CW + 40] = f(rnn_conv_w)[0].reshape(4, NRC, 128).transpose(2, 1, 0).reshape(128, 40)
    pv[:, PV_CB:PV_CB + 10] = f(rnn_conv_b)[0].reshape(NRC, 128).T
    pv[:, PV_BA:PV_BA + 10] = f(rnn_b_a)[0].reshape(NRC, 128).T
    pv[:, PV_BX:PV_BX + 10] = f(rnn_b_x)[0].reshape(NRC, 128).T
    pv[:, PV_LAM:PV_LAM + 10] = f(rnn_lambda)[0].reshape(NRC, 128).T
    pv[:, PV_QG] = np.tile(f(att_q_gain)[0], 2)
    pv[:, PV_KG] = np.tile(f(att_k_gain)[0], 2)
    rb = np.concatenate([f(rel_bias), np.full((1, 48), NEG, np.float32)], 0)
    bk = _bucket_table()
    bt = np.empty((128, 48, 256), np.float32)
    for g in range(3):
        bt[:, g * 16:(g + 1) * 16, :] = rb[bk[g]][:, :, g * 16:(g + 1) * 16].transpose(0, 2, 1)
    shared = {
        "pv": pv, "bt": bt,
        "ffn_w_in": f(ffn_w_in).reshape(4 * D, 2 * DFF),
        "ffn_w_out": f(ffn_w_out).reshape(4 * DFF, D),
        "rnn_w_in": f(rnn_w_in).reshape(D, 2 * DRNN),
        "rnn_w_a": f(rnn_w_a).reshape(DRNN, 128),
        "rnn_w_x": f(rnn_w_x).reshape(DRNN, 128),
        "rnn_w_out": f(rnn_w_out).reshape(DRNN, D),
        "att_w_qkv": f(att_w_qkv).reshape(D, 9216),
        "att_w_o": f(att_w_o).reshape(D, D),
    }
    in_maps = []
    for c in range(N_CORES):
        m = dict(shared)
        m["xT"] = np.ascontiguousarray(x[c].T)
        in_maps.append(m)
    if "nc" not in _CACHE:
        _CACHE["nc"] = build_program()
    res = run_bass_kernel_spmd(_CACHE["nc"], in_maps, core_ids=list(range(N_CORES)))
    out = np.stack([np.ascontiguousarray(res.results[c]["outT"].T) for c in range(N_CORES)], 0)
    return out.astype(np.float32)
```

```python
import math
from contextlib import ExitStack

import numpy as np
import concourse.bass as bass
import concourse.mybir as mybir
from concourse.bass_utils import run_bass_kernel_spmd

F32 = mybir.dt.float32
BF16 = mybir.dt.bfloat16
AF = mybir.ActivationFunctionType
ALU = mybir.AluOpType

D = 1024
S = 2048
KD = D // 128
NT = S // 512
DFF = 2816
NFC = DFF // 128
DRNN = 1280
NRC = DRNN // 128
EPS = 1e-6
NEG = -30000.0
DIL = (1, 4, 16)
N_CORES = 8

PV_G = 0
PV_CW = 48
PV_CB = 88
PV_BA = 98
PV_BX = 108
PV_LAM = 118
PV_QG = 128
PV_KG = 129
PV_N = 130


class Prog:
    ENG = ("pe", "act", "dve", "pool", "sp")
    CENG = ("pe", "act", "dve", "pool")

    def __init__(self, nc, es):
        self.nc = nc
        self.es = es
        self.q = {e: [] for e in self.ENG}
        self.seq = {e: 0 for e in self.ENG}
        self.waited = {e: {} for e in self.ENG}
        self.lastw = {}
        self.readers = {}
        self.sems = {}
        self.dcount = {}
        self.needed = {e: set() for e in self.CENG}
        for e in self.CENG:
            self.sems[e] = es.enter_context(nc.semaphore("s_" + e))

    def _sem(self, sk):
        if sk not in self.sems:
            self.sems[sk] = self.es.enter_context(self.nc.semaphore("d_" + sk))
        return self.sems[sk]

    def _wait(self, eng, sk, v):
        if self.waited[eng].get(sk, 0) >= v:
            return
        self.waited[eng][sk] = v
        self.q[eng].append(("wait", sk, v))
        if sk in self.needed:
            self.needed[sk].add(v)

    def _deps(self, eng, reads, writes):
        deps = {}

        def add(d):
            if d is None:
                return
            sk, v = d
            if deps.get(sk, 0) < v:
                deps[sk] = v

        for k in reads:
            add(self.lastw.get(k))
        for k in writes:
            add(self.lastw.get(k))
            for sk, v in self.readers.get(k, {}).items():
                add((sk, v))
        for sk, v in deps.items():
            if sk == eng and eng == "pe":
                continue
            self._wait(eng, sk, v)

    def _mark(self, sk, val, reads, writes):
        for k in writes:
            self.lastw[k] = (sk, val)
            self.readers[k] = {}
        for k in reads:
            r = self.readers.setdefault(k, {})
            if r.get(sk, 0) < val:
                r[sk] = val

    def op(self, eng, name, kw, reads=(), writes=(), sig=True):
        self._deps(eng, reads, writes)
        if sig:
            self.seq[eng] += 1
            oid = self.seq[eng]
        else:
            oid = None
        self.q[eng].append(("op", (name, kw), oid, sig))
        self._mark(eng, self.seq[eng] if sig else self.seq[eng] + 1, reads, writes)

    def barrier(self):
        for eng in self.ENG:
            for f in self.CENG:
                if f == eng or self.seq[f] == 0:
                    continue
                self._wait(eng, f, self.seq[f])

    def dma(self, eng, stream, out, in_, reads=(), writes=()):
        self._sem(stream)
        self._deps(eng, reads, writes)
        self.dcount[stream] = self.dcount.get(stream, 0) + 16
        self.q[eng].append(("dma", out, in_, stream))
        self._mark(stream, self.dcount[stream], reads, writes)

    def final_wait(self, eng, stream):
        self.q[eng].append(("wait", stream, self.dcount[stream]))

    def _rank(self, eng):
        ids = sorted(self.needed[eng])
        return {v: i + 1 for i, v in enumerate(ids)}

    def emit(self, eng, e):
        my = self.sems.get(eng)
        ranks = {f: self._rank(f) for f in self.CENG}
        myneed = self.needed.get(eng, set())
        for item in self.q[eng]:
            if item[0] == "wait":
                sk, v = item[1], item[2]
                if sk in ranks:
                    v = ranks[sk][v]
                e.wait_ge(self._sem(sk), v)
            elif item[0] == "op":
                name, kw = item[1]
                ins = getattr(e, name)(**kw)
                if item[2] is not None and item[2] in myneed:
                    ins.then_inc(my, 1)
            else:
                e.dma_start(out=item[1], in_=item[2]).then_inc(self._sem(item[3]), 16)


class Ring:
    def __init__(self, n):
        self.n = n
        self.i = -1

    def next(self):
        self.i = (self.i + 1) % self.n
        return self.i


def build_program():
    nc = bass.Bass("TRN2", target_bir_lowering=False)

    def din(name, shape):
        return nc.dram_tensor(name, list(shape), F32, kind="ExternalInput").ap()

    xT_d = din("xT", [D, S])
    pv_d = din("pv", [128, PV_N])
    bt_d = din("bt", [128, 48, 256])
    fwi_d = din("ffn_w_in", [4 * D, 2 * DFF])
    fwo_d = din("ffn_w_out", [4 * DFF, D])
    rwi_d = din("rnn_w_in", [D, 2 * DRNN])
    rwa_d = din("rnn_w_a", [DRNN, 128])
    rwx_d = din("rnn_w_x", [DRNN, 128])
    rwo_d = din("rnn_w_out", [DRNN, D])
    aqkv_d = din("att_w_qkv", [D, 9216])
    awo_d = din("att_w_o", [D, D])
    out_d = nc.dram_tensor("outT", [D, S], F32, kind="ExternalOutput").ap()

    with ExitStack() as es:
        P = Prog(nc, es)

        def sb(name, shape, dt):
            return es.enter_context(nc.sbuf_tensor(name, list(shape), dt))

        xT = sb("xT_sb", [128, KD, S], F32)
        xn = sb("xn_sb", [128, KD, S], BF16)
        pv = sb("pv_sb", [128, PV_N], F32)
        dv = sb("dv_sb", [128, 64], F32)
        ones_bf = sb("ones_bf", [128, 128], BF16)
        bd_bf = sb("bd_bf", [128, 128], BF16)
        banks = [es.enter_context(nc.psum_tensor("bank%d" % i, [128, 512], F32)) for i in range(8)]
        DV_EPS, DV_ONE, DV_NLN8, DV_NEG1, DV_HBA, DV_HBX, DV_C4, DV_TMP = 0, 1, 2, 3, 4, 14, 24, 34

        bank_rr = Ring(8)

        def BK(i):
            return ("bank", i)

        def MM(out, lhsT, rhs, start, stop, reads, writes, sig=True):
            P.op("pe", "matmul", dict(out=out, lhsT=lhsT, rhs=rhs, start=start, stop=stop), reads, writes, sig)

        def ACT(out, in_, func, reads, writes, bias=None, scale=None):
            kw = dict(out=out, in_=in_, func=func)
            if bias is not None:
                kw["bias"] = bias
            if scale is not None:
                kw["scale"] = scale
            P.op("act", "activation", kw, reads, writes)

        def TS(out, in0, s1, s2, op0, op1, reads, writes):
            kw = dict(out=out, in0=in0, scalar1=s1, scalar2=s2, op0=op0)
            if op1 is not None:
                kw["op1"] = op1
            P.op("dve", "tensor_scalar", kw, reads, writes)

        def STT(out, in0, scalar, in1, op0, op1, reads, writes):
            P.op("dve", "scalar_tensor_tensor", dict(out=out, in0=in0, scalar=scalar, in1=in1, op0=op0, op1=op1),
                 reads, writes)

        def TT(out, in0, in1, op, reads, writes):
            P.op("dve", "tensor_tensor", dict(out=out, in0=in0, in1=in1, op=op), reads, writes)

        def RECIP(out, in_, reads, writes):
            P.op("dve", "reciprocal", dict(out=out, in_=in_), reads, writes)

        def RECIP_FAST(out, in_, reads, writes):
            P.op("dve", "reciprocal_approx_fast", dict(out=out, in_=in_), reads, writes)

        def MEMSET(ap, val, writes):
            P.op("dve", "memset", dict(ap=ap, constant=val), (), writes)

        def col(t, c, n=1):
            return t[:, c:c + n]

        xT_dv = xT_d.rearrange("(k p) t -> p k t", p=128)
        P.dma("sp", "pvin", pv[:], pv_d, writes=["pv"])
        for t in range(NT):
            P.dma("sp", "xin%d" % t, xT[:, :, t * 512:(t + 1) * 512], xT_dv[:, :, t * 512:(t + 1) * 512],
                  writes=[("x", k, t) for k in range(KD)])

        MEMSET(ones_bf[:], 1.0, ["ones"])
        MEMSET(bd_bf[:], 0.0, ["bd"])
        MEMSET(bd_bf[0:64, 0:64], 1.0, ["bd"])
        MEMSET(bd_bf[64:128, 64:128], 1.0, ["bd"])
        MEMSET(col(dv, DV_EPS), EPS, ["dvc"])
        MEMSET(col(dv, DV_ONE), 1.0, ["dvc"])
        MEMSET(col(dv, DV_NLN8), -math.log(8.0), ["dvc"])
        MEMSET(col(dv, DV_NEG1), -1.0, ["dvc"])
        TS(col(dv, DV_HBA, 20), col(pv, PV_BA, 20), 0.5, None, ALU.mult, None, ["pv"], ["dvh"])
        ACT(col(dv, DV_TMP, 10), col(pv, PV_LAM, 10), AF.Exp, ["pv"], ["dvt"], scale=-1.0)
        ACT(col(dv, DV_TMP + 10, 10), col(dv, DV_TMP, 10), AF.Ln, ["dvt", "dvc"], ["dvt2"], bias=col(dv, DV_ONE), scale=1.0)
        TS(col(dv, DV_C4, 10), col(dv, DV_TMP + 10, 10), 2.0, None, ALU.mult, None, ["dvt2"], ["dvc4"])

        TSL = [slice(t * 512, (t + 1) * 512) for t in range(NT)]

        def rmsnorm(gcol, tmp):
            sq, lnt, rs = tmp
            sq_rr, ln_rr = Ring(4), Ring(2)
            for t in range(NT):
                ts = TSL[t]
                b = bank_rr.next()
                for k in range(KD):
                    s = sq_rr.next()
                    ACT(sq[:, s, :], xT[:, k, ts], AF.Square, [("x", k, t)], [("sq", s)])
                    MM(banks[b][:, :], ones_bf[:, :], sq[:, s, :], k == 0, k == KD - 1,
                       [("sq", s), "ones"], [BK(b)])
                l = ln_rr.next()
                ACT(lnt[:, l, :], banks[b][:, :], AF.Ln, ["dvc"], [BK(b), ("lnt", l)], bias=col(dv, DV_EPS), scale=1.0 / D)
                ACT(rs[:, l, :], lnt[:, l, :], AF.Exp, [("lnt", l)], [("rs", l)], scale=-0.5)
                for k in range(KD):
                    STT(xn[:, k, ts], xT[:, k, ts], col(pv, gcol + k), rs[:, l, :], ALU.mult, ALU.mult,
                        [("x", k, t), ("rs", l), "pv"], [("xn", k, t)])

        def ffn(idx, gcol):
            with ExitStack() as fs:
                def fsb(name, shape, dt):
                    return fs.enter_context(nc.sbuf_tensor("%s_%d" % (name, idx), list(shape), dt))
                NSL = 2
                CPS = NFC // NSL
                gT = fsb("gT", [128, CPS, S], BF16)
                win = fsb("win", [128, 4, 2, KD, 128], BF16)
                wout = fsb("wout", [128, 4, CPS, 128], BF16)
                sq = fsb("sq", [128, 4, 512], BF16)
                lnt = fsb("lnt", [128, 2, 512], F32)
                rs = fsb("rs", [128, 2, 512], F32)
                sg = fsb("sg", [128, 3, 512], F32)
                rmsnorm(gcol, (sq, lnt, rs))
                if DBG.get("ffn_stop") == "norm":
                    P.barrier()
                    return
                wi_v = fwi_d[idx * D:(idx + 1) * D, :].rearrange("(k p) f -> p k f", p=128)
                wo_v = fwo_d[idx * DFF:(idx + 1) * DFF, :].rearrange("(c p) d -> p c d", p=128)
                win_rr, wout_rr, sg_rr = Ring(4), Ring(4), Ring(3)
                def load_win(c):
                    ws = win_rr.next()
                    for gu in range(2):
                        c0 = gu * DFF + c * 128
                        P.dma("pool", "win%d_%d" % (ws, gu), win[:, ws, gu, :, :], wi_v[:, :, c0:c0 + 128],
                              writes=[("win", ws, gu)])
                    return ws

                def up_unit(ci, t, ws):
                    ts = TSL[t]
                    bg = bank_rr.next()
                    bu = bank_rr.next()
                    for gu, b in ((0, bg), (1, bu)):
                        for k in range(KD):
                            MM(banks[b][:, :], win[:, ws, gu, k, :], xn[:, k, ts], k == 0, k == KD - 1,
                               [("win", ws, gu), ("xn", k, t)], [BK(b)], sig=(k == KD - 1))
                    s = sg_rr.next()
                    ACT(sg[:, s, :], banks[bg][:, :], AF.Silu, [], [BK(bg), ("sg", s)])
                    TT(gT[:, ci, ts], sg[:, s, :], banks[bu][:, :], ALU.mult, [("sg", s)], [BK(bu), ("gT", ci, t)])

                for sl in range(NSL):
                    ci0 = 0
                    if sl == 0:
                        wsl0 = [load_win(sl * CPS + ci) for ci in range(3)]
                        for t in range(NT):
                            for ci in range(3):
                                up_unit(ci, t, wsl0[ci])
                        ci0 = 3
                    for ci in range(ci0, CPS):
                        ws = load_win(sl * CPS + ci)
                        for t in range(NT):
                            up_unit(ci, t, ws)
                    if DBG.get("ffn_stop") == "up":
                        continue
                    for dc in range(KD):
                        ws = wout_rr.next()
                        for hf, (c_lo, c_hi) in enumerate(((0, 6), (6, CPS))):
                            P.dma("pool", "wout%d_%d" % (ws, hf), wout[:, ws, c_lo:c_hi, :],
                                  wo_v[:, sl * CPS + c_lo:sl * CPS + c_hi, dc * 128:(dc + 1) * 128],
                                  writes=[("wout", ws, hf)])
                        for t in range(NT):
                            ts = TSL[t]
                            b = bank_rr.next()
                            for ci in range(CPS):
                                MM(banks[b][:, :], wout[:, ws, ci, :], gT[:, ci, ts], ci == 0, ci == CPS - 1,
                                   [("wout", ws, 0 if ci < 6 else 1), ("gT", ci, t)], [BK(b)], sig=(ci == CPS - 1))
                            STT(xT[:, dc, ts], banks[b][:, :], 0.5, xT[:, dc, ts], ALU.mult, ALU.add,
                                [], [BK(b), ("x", dc, t)])
                P.barrier()

        def rnn(gcol):
            with ExitStack() as fs:
                def fsb(name, shape, dt):
                    return fs.enter_context(nc.sbuf_tensor("r_" + name, list(shape), dt))
                sq = fsb("sq", [128, 4, 512], BF16)
                lnt = fsb("lnt", [128, 2, 512], F32)
                rs = fsb("rs", [128, 2, 512], F32)
                wgu = fsb("wgu", [128, 3, 2, KD, 128], BF16)
                wax = fsb("wax", [128, NRC, 2, 128], BF16)
                wo = fsb("wo", [128, NRC, D], BF16)
                gg = fsb("gg", [128, 3, 512], F32)
                ub = fsb("ub", [128, 2, 516], F32)
                cv = fsb("cv", [128, 2, 512], F32)
                cvb = fsb("cvb", [128, 2, 512], BF16)
                rp = fsb("rp", [128, 2, 512], F32)
                ip = fsb("ip", [128, 2, 512], F32)
                uu = fsb("uu", [128, 2, 512], F32)
                ww = fsb("ww", [128, 2, 512], F32)
                yb = fsb("yb", [128, 2, NRC, 512], BF16)
                ucar = fsb("ucar", [128, NRC, 4], F32)
                hcar = fsb("hcar", [128, NRC], F32)
                rmsnorm(gcol, (sq, lnt, rs))
                MEMSET(ucar[:, :, :], 0.0, [("ucar", n) for n in range(NRC)])
                wa_v = rwa_d.rearrange("(n p) d -> p n d", p=128)
                wx_v = rwx_d.rearrange("(n p) d -> p n d", p=128)
                wo_v = rwo_d.rearrange("(n p) d -> p n d", p=128)
                for hf in range(2):
                    ns = slice(hf * 5, hf * 5 + 5)
                    P.dma("pool", "rwa%d" % hf, wax[:, ns, 0, :], wa_v[:, ns, :], writes=[("wax", 0, hf)])
                    P.dma("pool", "rwx%d" % hf, wax[:, ns, 1, :], wx_v[:, ns, :], writes=[("wax", 1, hf)])
                    P.dma("pool", "rwo%d" % hf, wo[:, ns, :], wo_v[:, ns, :], writes=[("wo", hf)])
                wi_v = rwi_d.rearrange("(k p) f -> p k f", p=128)
                NU = NT * NRC
                units = [(t, n) for t in range(NT) for n in range(NRC)]

                def load_w(k):
                    t, n = units[k]
                    ws = k % 3
                    for gu in range(2):
                        c0 = gu * DRNN + n * 128
                        P.dma("pool", "rwgu%d_%d" % (ws, gu), wgu[:, ws, gu, :, :], wi_v[:, :, c0:c0 + 128],
                              writes=[("wgu", ws, gu)])

                def stage_a(k):
                    t, n = units[k]
                    pb, ws, ts = k % 2, k % 3, TSL[t]
                    b = bank_rr.next()
                    for kd in range(KD):
                        MM(banks[b][:, :], wgu[:, ws, 0, kd, :], xn[:, kd, ts], kd == 0, kd == KD - 1,
                           [("wgu", ws, 0), ("xn", kd, t)], [BK(b)], sig=(kd == KD - 1))
                    ACT(gg[:, k % 3, :], banks[b][:, :], AF.Gelu_apprx_tanh, [], [BK(b), ("gg", k % 3)])
                    b = bank_rr.next()
                    for kd in range(KD):
                        MM(banks[b][:, :], wgu[:, ws, 1, kd, :], xn[:, kd, ts], kd == 0, kd == KD - 1,
                           [("wgu", ws, 1), ("xn", kd, t)], [BK(b)], sig=(kd == KD - 1))
                    ACT(ub[:, pb, 3:515], banks[b][:, :], AF.Copy, [], [BK(b), ("ub", pb)])
                    ACT(ub[:, pb, 0:3], ucar[:, n, 0:3], AF.Copy, [("ucar", n)], [("ub", pb)])
                    ACT(ucar[:, n, 0:3], ub[:, pb, 512:515], AF.Copy, [("ub", pb)], [("ucar", n)])

                def stage_c1(k):
                    t, n = units[k]
                    pb = k % 2
                    cw = PV_CW + n * 4
                    TS(cv[:, pb, :], ub[:, pb, 0:512], col(pv, cw), col(pv, PV_CB + n), ALU.mult, ALU.add,
                       [("ub", pb), "pv"], [("cv", pb)])
                    for kk in range(1, 4):
                        STT(cv[:, pb, :], ub[:, pb, kk:kk + 512], col(pv, cw + kk), cv[:, pb, :], ALU.mult, ALU.add,
                            [("ub", pb), "pv"], [("cv", pb)])

                def stage_c2(k):
                    t, n = units[k]
                    pb = k % 2
                    ACT(cvb[:, pb, :], cv[:, pb, :], AF.Copy, [("cv", pb)], [("cvb", pb)])
                    br = bank_rr.next()
                    MM(banks[br][:, :], wax[:, n, 0, :], cvb[:, pb, :], True, True, [("wax", 0, n // 5), ("cvb", pb)], [BK(br)])
                    bi = bank_rr.next()
                    MM(banks[bi][:, :], wax[:, n, 1, :], cvb[:, pb, :], True, True, [("wax", 1, n // 5), ("cvb", pb)], [BK(bi)])
                    ACT(rp[:, pb, :], banks[br][:, :], AF.Tanh, ["dvh"], [BK(br), ("rp", pb)], bias=col(dv, DV_HBA + n), scale=0.5)
                    ACT(ip[:, pb, :], banks[bi][:, :], AF.Tanh, ["dvh"], [BK(bi), ("ip", pb)], bias=col(dv, DV_HBX + n), scale=0.5)
                    ACT(uu[:, pb, :], rp[:, pb, :], AF.Tanh, [("rp", pb), "dvc4"], [("uu", pb)],
                        bias=col(dv, DV_C4 + n), scale=col(dv, DV_C4 + n))
                    ACT(ww[:, pb, :], uu[:, pb, :], AF.Identity, [("uu", pb), "dvc"], [("ww", pb)], bias=col(dv, DV_ONE), scale=1.0)

                def stage_e1(k):
                    pb = k % 2
                    RECIP(ww[:, pb, :], ww[:, pb, :], [("ww", pb)], [("ww", pb)])

                def stage_e2(k):
                    t, n = units[k]
                    pb = k % 2
                    ACT(rp[:, pb, :], ww[:, pb, :], AF.Identity, [("ww", pb), "dvc"], [("rp", pb)], bias=col(dv, DV_NEG1), scale=2.0)
                    ACT(uu[:, pb, :], uu[:, pb, :], AF.Sqrt, [("uu", pb)], [("uu", pb)])
                    STT(ip[:, pb, :], ip[:, pb, :], 1.0, cv[:, pb, :], ALU.add, ALU.mult, [("ip", pb), ("cv", pb)], [("ip", pb)])
                    P.op("pool", "tensor_tensor", dict(out=uu[:, pb, :], in0=uu[:, pb, :], in1=ww[:, pb, :], op=ALU.mult),
                         [("uu", pb), ("ww", pb)], [("uu", pb)])

                def stage_e3(k):
                    t, n = units[k]
                    pb = k % 2
                    TT(ip[:, pb, :], ip[:, pb, :], uu[:, pb, :], ALU.mult, [("uu", pb), ("ip", pb)], [("ip", pb)])
                    init = 0.0 if t == 0 else hcar[:, n:n + 1]
                    P.op("dve", "tensor_tensor_scan",
                         dict(out=cv[:, pb, :], data0=rp[:, pb, :], data1=ip[:, pb, :], initial=init, op0=ALU.mult, op1=ALU.add),
                         [("rp", pb), ("ip", pb), ("hcar", n)], [("cv", pb)])
                    if t + 1 < NT:
                        ACT(hcar[:, n:n + 1], cv[:, pb, 511:512], AF.Copy, [("cv", pb)], [("hcar", n)])
                    P.op("pool", "tensor_tensor", dict(out=yb[:, t % 2, n, :], in0=cv[:, pb, :], in1=gg[:, k % 3, :], op=ALU.mult),
                         [("cv", pb), ("gg", k % 3)], [("yb", t % 2, n)])
                    if n == NRC - 1:
                        for dc in range(KD):
                            b = bank_rr.next()
                            for m in range(NRC):
                                MM(banks[b][:, :], wo[:, m, dc * 128:(dc + 1) * 128], yb[:, t % 2, m, :], m == 0, m == NRC - 1,
                                   [("wo", m // 5), ("yb", t % 2, m)], [BK(b)], sig=(m == NRC - 1))
                            TT(xT[:, dc, TSL[t]], banks[b][:, :], xT[:, dc, TSL[t]], ALU.add, [], [BK(b), ("x", dc, t)])

                for k in range(min(3, NU)):
                    load_w(k)
                stage_a(0)
                for k in range(NU + 1):
                    if k + 3 < NU:
                        load_w(k + 3)
                    if k + 1 < NU:
                        stage_a(k + 1)
                    if k - 1 >= 0:
                        stage_e1(k - 1)
                    if k < NU:
                        stage_c1(k)
                    if k - 1 >= 0:
                        stage_e2(k - 1)
                    if k < NU:
                        stage_c2(k)
                    if k - 1 >= 0:
                        stage_e3(k - 1)
                P.barrier()

        def attention(gcol):
            with ExitStack() as fs:
                def fsb(name, shape, dt):
                    return fs.enter_context(nc.sbuf_tensor("a_" + name, list(shape), dt))
                sq = fsb("sq", [128, 4, 512], BF16)
                lnt = fsb("lnt", [128, 2, 512], F32)
                rs = fsb("rs", [128, 2, 512], F32)
                wqkv = fsb("wqkv", [128, 2, 3, KD, 128], BF16)
                wo = fsb("wo", [128, D], BF16)
                btb = fsb("btb", [128, 3, 2, 256], F32)
                qz = fsb("qz", [128, 2, 2, S], BF16)
                qs = fsb("qs", [128, S], BF16)
                kT = fsb("kT", [128, 2, S], BF16)
                va = fsb("va", [128, 2, 16, 2, 128], BF16)
                oacc = fsb("oacc", [128, 2, S], F32)
                rec = fsb("rec", [128, 2, 512], F32)
                oT = fsb("oT", [128, S], BF16)
                st = fsb("st", [128, 3, 2, 256], F32)
                pt = fsb("pt", [128, 3, 2, 256], BF16)
                rmsnorm(gcol, (sq, lnt, rs))
                MEMSET(qz[:, :, :, :], 0.0, [("qk", 0, 0), ("qk", 0, 1)])
                for vb in range(2):
                    MEMSET(va[:, vb, :, 0, 64:128], 1.0, [("vaones", vb)])
                    MEMSET(va[:, vb, :, 1, 0:64], 1.0, [("vaones", vb)])
                wq_v = aqkv_d.rearrange("(k p) f -> p k f", p=128)
                sq_rr, ln_rr, st_rr = Ring(4), Ring(2), Ring(3)
                S_BANKS = Ring(2)
                PJ_BANKS = Ring(2)
                FIN_BANKS = Ring(2)
                SS_BANK, V_BANK = 6, 7
                units = [(p, g) for p in range(8) for g in range(3)]

                def load_unit(u):
                    p, g = units[u]
                    pg = u % 2
                    for qi in range(3):
                        c0 = (qi * 3 + g) * 1024 + p * 128
                        P.dma("pool", "wqkv%d_%d" % (pg, qi), wqkv[:, pg, qi, :, :], wq_v[:, :, c0:c0 + 128],
                              writes=[("wqkv", pg, qi)])
                    P.dma("sp", "bt%d" % (u % 3), btb[:, u % 3, :, :], bt_d[:, g * 16 + 2 * p:g * 16 + 2 * p + 2, :],
                          writes=[("bt", u % 3)])

                def proj_steps(u):
                    p, g = units[u]
                    pg = u % 2
                    d = DIL[g]
                    nb = (S // d) // 128
                    if u + 1 < len(units):
                        load_unit(u + 1)
                    tiles = [(qi, t) for qi in range(2) for t in range(NT)]
                    info = {}

                    def st_mm(j):
                        qi, t = tiles[j]
                        b = 4 + PJ_BANKS.next()
                        for k in range(KD):
                            MM(banks[b][:, :], wqkv[:, pg, qi, k, :], xn[:, k, TSL[t]], k == 0, k == KD - 1,
                               [("wqkv", pg, qi), ("xn", k, t)], [BK(b)], sig=(k == KD - 1))
                        info[j] = [b, None, None]

                    def st_sq(j):
                        b = info[j][0]
                        s = sq_rr.next()
                        ACT(sq[:, s, :], banks[b][:, :], AF.Square, [], [BK(b), ("sq", s)])
                        info[j][1] = s

                    def st_ss(j):
                        s = info[j][1]
                        MM(banks[SS_BANK][:, :], bd_bf[:, :], sq[:, s, :], True, True, [("sq", s), "bd"], [BK(SS_BANK)])

                    def st_le(j):
                        qi, t = tiles[j]
                        l = ln_rr.next()
                        ACT(lnt[:, l, :], banks[SS_BANK][:, :], AF.Ln, ["dvc"], [BK(SS_BANK), ("lnt", l)],
                            bias=col(dv, DV_EPS), scale=1.0 / 64)
                        ACT(rs[:, l, :], lnt[:, l, :], AF.Exp, [("lnt", l), "dvc"], [("rs", l)],
                            bias=(col(dv, DV_NLN8) if qi == 0 else None), scale=-0.5)
                        info[j][2] = l

                    def st_out(j):
                        qi, t = tiles[j]
                        b, s, l = info[j]
                        gain = col(pv, PV_QG) if qi == 0 else col(pv, PV_KG)
                        lc = 512 // d
                        if qi == 0:
                            parts = [(slice(0, 64), qz[0:64, pg, 0, :]), (slice(64, 128), qz[64:128, pg, 1, :])]
                        else:
                            parts = [(slice(0, 128), kT[:, pg, :])]
                        def region(full):
                            if d == 1:
                                return full[:, TSL[t]]
                            return full.rearrange("p (r l) -> p r l", r=d)[:, :, t * lc:(t + 1) * lc]

                        if qi == 0:
                            if d == 1:
                                i_ap, r_ap = banks[b][:, :], rs[:, l, :]
                            else:
                                i_ap = banks[b][:, :].rearrange("p (l r) -> p r l", r=d)
                                r_ap = rs[:, l, :].rearrange("p (l r) -> p r l", r=d)
                            qkeys = [("qs", t % 2)] + ([("qs", 1)] if t == 0 else [])
                            STT(region(qs[:, :]), i_ap, gain, r_ap, ALU.mult, ALU.mult, [("rs", l), "pv"], [BK(b)] + qkeys)
                            P.op("pool", "tensor_copy", dict(out=region(qz[0:64, pg, 0, :]), in_=region(qs[0:64, :])),
                                 [("qs", t % 2)], [("qk", 0, pg)])
                            P.op("pool", "tensor_copy", dict(out=region(qz[64:128, pg, 1, :]), in_=region(qs[64:128, :])),
                                 [("qs", t % 2)], [("qk", 0, pg)])
                            return
                        for ps_, dfull in parts:
                            if d == 1:
                                o_ap = dfull[:, TSL[t]]
                                i_ap = banks[b][ps_, :]
                                r_ap = rs[ps_, l, :]
                            else:
                                o_ap = dfull.rearrange("p (r l) -> p r l", r=d)[:, :, t * lc:(t + 1) * lc]
                                i_ap = banks[b][ps_, :].rearrange("p (l r) -> p r l", r=d)
                                r_ap = rs[ps_, l, :].rearrange("p (l r) -> p r l", r=d)
                            STT(o_ap, i_ap, gain[ps_, :], r_ap, ALU.mult, ALU.mult, [("rs", l), "pv"], [BK(b), ("qk", qi, pg)])

                    nt = len(tiles)
                    for j in range(nt + 1):
                        if 0 <= j - 1 < nt:
                            st_sq(j - 1)
                        yield
                        if 0 <= j - 1 < nt:
                            st_ss(j - 1)
                        if j < nt:
                            st_mm(j)
                        if 0 <= j - 1 < nt:
                            st_le(j - 1)
                            st_out(j - 1)
                        yield
                    for bq in range(4):
                        b = 4 + PJ_BANKS.next()
                        for bl in range(4):
                            B = 4 * bq + bl
                            r, jb = B // nb, B % nb
                            for k in range(KD):
                                if d == 1:
                                    lh = xn[:, k, B * 128:(B + 1) * 128]
                                else:
                                    lh = xn[:, k, :].rearrange("p (l r) -> p r l", r=d)[:, r, jb * 128:(jb + 1) * 128]
                                MM(banks[b][:, bl * 128:(bl + 1) * 128], lh, wqkv[:, pg, 2, k, :], k == 0, k == KD - 1,
                                   [("wqkv", pg, 2)] + [("xn", k, t) for t in range(NT)], [BK(b)],
                                   sig=(k == KD - 1 and bl == 3))
                        yield
                        for h in range(2):
                            ACT(va[:, pg, 4 * bq:4 * bq + 4, h, h * 64:(h + 1) * 64],
                                banks[b][:, :].rearrange("p (b c) -> p b c", b=4)[:, :, h * 64:(h + 1) * 64], AF.Copy,
                                [("vaones", pg)], [BK(b), ("va", pg, bq)])
                        yield

                NBLK = 16 * len(units)
                cinfo = {}

                def c_s(i):
                    u, B = i // 16, i % 16
                    p, g = units[u]
                    pg = u % 2
                    nb = (S // DIL[g]) // 128
                    jb = B % nb
                    sbk = 2 + (i % 2)
                    cs = slice(B * 128, (B + 1) * 128)
                    for h in range(2):
                        if jb > 0:
                            MM(banks[sbk][:, h * 256:h * 256 + 128], kT[:, pg, (B - 1) * 128:B * 128], qz[:, pg, h, cs],
                               True, True, [("qk", 0, pg), ("qk", 1, pg)], [BK(sbk)], sig=False)
                        MM(banks[sbk][:, h * 256 + 128:h * 256 + 256], kT[:, pg, cs], qz[:, pg, h, cs], True, True,
                           [("qk", 0, pg), ("qk", 1, pg)], [BK(sbk)], sig=(h == 1))

                def c_add(i):
                    u, B = i // 16, i % 16
                    p, g = units[u]
                    nb = (S // DIL[g]) // 128
                    lo = 0 if (B % nb) > 0 else 128
                    sbk = 2 + (i % 2)
                    si = i % 3
                    TT(st[:, si, :, lo:256], banks[sbk][:, :].rearrange("p (h c) -> p h c", h=2)[:, :, lo:256],
                       btb[:, u % 3, :, lo:256], ALU.add, [("bt", u % 3)], [BK(sbk), ("st", si)])

                def c_exp(i):
                    u, B = i // 16, i % 16
                    p, g = units[u]
                    nb = (S // DIL[g]) // 128
                    lo = 0 if (B % nb) > 0 else 128
                    si = i % 3
                    ACT(pt[:, si, :, lo:256], st[:, si, :, lo:256], AF.Exp, [("st", si)], [("pt", si)])

                def c_pv(i):
                    u, B = i // 16, i % 16
                    p, g = units[u]
                    pg = u % 2
                    d = DIL[g]
                    nb = (S // d) // 128
                    r, jb = B // nb, B % nb
                    si = i % 3
                    for h in range(2):
                        obk = h
                        oreg = banks[obk][:, (B % 4) * 128:(B % 4 + 1) * 128]
                        if jb > 0:
                            MM(oreg, va[:, pg, B - 1, h, :], pt[:, si, h, 0:128], True, False,
                               [("pt", si), ("va", pg, (B - 1) // 4), ("vaones", pg)], [BK(obk)], sig=False)
                        MM(oreg, va[:, pg, B, h, :], pt[:, si, h, 128:256], jb == 0, True,
                           [("pt", si), ("va", pg, B // 4), ("vaones", pg)], [BK(obk)])
                    if B % 4 == 3:
                        B0 = B - 3
                        for h in range(2):
                            obk = h
                            if d == 1:
                                o_ap = oacc[:, h, B0 * 128:(B0 + 4) * 128]
                                i_ap = banks[obk][:, :]
                            elif d == 4:
                                o_ap = oacc[:, h, :].rearrange("p (l r) -> p r l", r=4)[:, r, :]
                                i_ap = banks[obk][:, :]
                            else:
                                o_ap = oacc[:, h, :].rearrange("p (l r) -> p r l", r=16)[:, B0:B0 + 4, :]
                                i_ap = banks[obk][:, :].rearrange("p (r l) -> p r l", r=4)
                            if g == 0:
                                ACT(o_ap, i_ap, AF.Copy, [], [BK(obk), ("oacc", h)])
                            else:
                                TT(o_ap, i_ap, o_ap, ALU.add, [], [BK(obk), ("oacc", h)])

                def fin_steps(p):
                    P.dma("pool", "awo", wo[:, :], awo_d[p * 128:(p + 1) * 128, :], writes=[("awo",)])
                    for t in range(NT):
                        ts = TSL[t]
                        ri = t % 2
                        ACT(oacc[64:128, 0, ts], oacc[64:128, 0, ts], AF.Ln, [("oacc", 0)], [("oacc", 0)])
                        ACT(rec[0:64, ri, :], oacc[64:128, 0, ts], AF.Exp, [("oacc", 0)], [("rec", ri, 0)], scale=-1.0)
                        ACT(oacc[0:64, 1, ts], oacc[0:64, 1, ts], AF.Ln, [("oacc", 1)], [("oacc", 1)])
                        ACT(rec[64:128, ri, :], oacc[0:64, 1, ts], AF.Exp, [("oacc", 1)], [("rec", ri, 1)], scale=-1.0)
                        TT(oT[0:64, ts], oacc[0:64, 0, ts], rec[0:64, ri, :], ALU.mult, [("oacc", 0), ("rec", ri, 0)], [("oT", t, 0)])
                        TT(oT[64:128, ts], oacc[64:128, 1, ts], rec[64:128, ri, :], ALU.mult, [("oacc", 1), ("rec", ri, 1)], [("oT", t, 1)])
                        yield
                    for dc in range(KD):
                        for t in range(NT):
                            b = 7 if (p != 7 or (dc * NT + t) % 2 == 0) else SS_BANK
                            MM(banks[b][:, :], wo[:, dc * 128:(dc + 1) * 128], oT[:, TSL[t]], True, True,
                               [("awo",), ("oT", t, 0), ("oT", t, 1)], [BK(b)])
                            TT(xT[:, dc, TSL[t]], banks[b][:, :], xT[:, dc, TSL[t]], ALU.add, [], [BK(b), ("x", dc, t)])
                            yield

                load_unit(0)
                for _ in proj_steps(0):
                    pass
                projg = None
                fing = []

                def proj_next():
                    nonlocal projg
                    if projg is not None:
                        try:
                            next(projg)
                        except StopIteration:
                            projg = None

                for i in range(NBLK + 3):
                    if i < NBLK and i % 16 == 1 and i // 16 + 1 < len(units):
                        projg = proj_steps(i // 16 + 1)
                    proj_next()
                    if 0 <= i - 1 < NBLK:
                        c_add(i - 1)
                    if 0 <= i - 2 < NBLK:
                        c_exp(i - 2)
                    if 0 <= i - 3 < NBLK:
                        c_pv(i - 3)
                        u3, B3 = (i - 3) // 16, (i - 3) % 16
                        if B3 == 15 and units[u3][1] == 2:
                            fing.append((fin_steps(units[u3][0]), [0]))
                    if i < NBLK:
                        c_s(i)
                    proj_next()
                    if fing:
                        gen, cnt = fing[0]
                        for _ in range(1 if cnt[0] < NT else 3):
                            try:
                                next(gen)
                                cnt[0] += 1
                            except StopIteration:
                                fing.pop(0)
                                break
                for fg, _cnt in fing:
                    for _ in fg:
                        pass
                P.barrier()

        P.barrier()
        for ph in PHASES:
            if ph == "ffn0":
                ffn(0, PV_G + 0 * 8)
            elif ph == "rnn":
                rnn(PV_G + 1 * 8)
            elif ph == "ffn1":
                ffn(1, PV_G + 2 * 8)
            elif ph == "ffn2":
                ffn(2, PV_G + 3 * 8)
            elif ph == "att":
                attention(PV_G + 4 * 8)
            elif ph == "ffn3":
                ffn(3, PV_G + 5 * 8)

        out_dv = out_d.rearrange("(k p) t -> p k t", p=128)
        for k in range(KD):
            P.dma("sp", "xout", out_dv[:, k, :], xT[:, k, :], reads=[("x", k, t) for t in range(NT)])
        P.final_wait("sp", "xout")

        with nc.Block() as block:
            @block.tensor
            def _(e):
                P.emit("pe", e)

            @block.scalar
            def _(e):
                P.emit("act", e)

            @block.vector
            def _(e):
                P.emit("dve", e)

            @block.gpsimd
            def _(e):
                P.emit("pool", e)

            @block.sync
            def _(e):
                P.emit("sp", e)
    return nc


PHASES = ("ffn0", "rnn", "ffn1", "ffn2", "att", "ffn3")
DBG = {}


def _bucket_table():
    k = np.arange(128)[:, None]
    j = np.arange(256)[None, :]
    dist = np.where(j < 128, j + 128 - k, (j - 128) - k)
    valid = (dist >= 0) & (dist <= 128)
    tabs = []
    for d in DIL:
        n = np.maximum(dist * d, 0).astype(np.int32)
        nf = np.maximum(n, 1).astype(np.float32)
        large = 16 + (np.log(nf / np.float32(16)) / np.float32(math.log(2048 / 16)) * np.float32(16)).astype(np.int32)
        large = np.minimum(large, 31)
        bk = np.where(n < 16, n, large)
        tabs.append(np.where(valid, bk, 32))
    return np.stack(tabs, 0)


_CACHE = {}


def kernel(x, norm_g, ffn_w_in, ffn_w_out, rnn_w_in, rnn_conv_w, rnn_conv_b, rnn_w_a, rnn_b_a,
           rnn_w_x, rnn_b_x, rnn_lambda, rnn_w_out, att_w_qkv, att_q_gain, att_k_gain, att_w_o, rel_bias):
    f = lambda a: np.ascontiguousarray(np.asarray(a, dtype=np.float32))
    x = f(x)
    pv = np.zeros((128, PV_N), np.float32)
    pv[:, PV_G:PV_G + 48] = f(norm_g).reshape(6, KD, 128).transpose(2, 0, 1).reshape(128, 48)
    pv[:, PV_CW:PV_CW + 40] = f(rnn_conv_w)[0].reshape(4, NRC, 128).transpose(2, 1, 0).reshape(128, 40)
    pv[:, PV_CB:PV_CB + 10] = f(rnn_conv_b)[0].reshape(NRC, 128).T
    pv[:, PV_BA:PV_BA + 10] = f(rnn_b_a)[0].reshape(NRC, 128).T
    pv[:, PV_BX:PV_BX + 10] = f(rnn_b_x)[0].reshape(NRC, 128).T
    pv[:, PV_LAM:PV_LAM + 10] = f(rnn_lambda)[0].reshape(NRC, 128).T
    pv[:, PV_QG] = np.tile(f(att_q_gain)[0], 2)
    pv[:, PV_KG] = np.tile(f(att_k_gain)[0], 2)
    rb = np.concatenate([f(rel_bias), np.full((1, 48), NEG, np.float32)], 0)
    bk = _bucket_table()
    bt = np.empty((128, 48, 256), np.float32)
    for g in range(3):
        bt[:, g * 16:(g + 1) * 16, :] = rb[bk[g]][:, :, g * 16:(g + 1) * 16].transpose(0, 2, 1)
    shared = {
        "pv": pv, "bt": bt,
        "ffn_w_in": f(ffn_w_in).reshape(4 * D, 2 * DFF),
        "ffn_w_out": f(ffn_w_out).reshape(4 * DFF, D),
        "rnn_w_in": f(rnn_w_in).reshape(D, 2 * DRNN),
        "rnn_w_a": f(rnn_w_a).reshape(DRNN, 128),
        "rnn_w_x": f(rnn_w_x).reshape(DRNN, 128),
        "rnn_w_out": f(rnn_w_out).reshape(DRNN, D),
        "att_w_qkv": f(att_w_qkv).reshape(D, 9216),
        "att_w_o": f(att_w_o).reshape(D, D),
    }
    in_maps = []
    for c in range(N_CORES):
        m = dict(shared)
        m["xT"] = np.ascontiguousarray(x[c].T)
        in_maps.append(m)
    if "nc" not in _CACHE:
        _CACHE["nc"] = build_program()
    res = run_bass_kernel_spmd(_CACHE["nc"], in_maps, core_ids=list(range(N_CORES)))
    out = np.stack([np.ascontiguousarray(res.results[c]["outT"].T) for c in range(N_CORES)], 0)
    return out.astype(np.float32)
```
